# Optimizing a Trainium2 kernel written in Bass

```python
import math
import jax, jax.numpy as jnp
from jax import lax
import numpy as np

D_MODEL = 2048
BATCH = 1
SEQ = 8192
DEPTH = 1

A_GROUPS = 16
A_GROUP_DIM = 128
A_WIDTH = A_GROUPS * A_GROUP_DIM
A_CHUNK = 128
B_HEADS = 16
B_HEAD_DIM = 128
B_WIDTH = B_HEADS * B_HEAD_DIM
MOBA_BLOCK = 256
MOBA_TOPK = 3
Q_CHUNK = 32
REL_BUCKETS = 32
REL_MAX_DIST = 128
D_FF = 4 * D_MODEL
EPS = 1e-6
NEG = -1e30
IN_COLS = 2 * A_WIDTH + 3 * B_WIDTH + 2 * D_MODEL

kernel_name = "hybrid_gmlp_moba_gated_block"


def rmsnorm(x, g):
    xf = x.astype(jnp.float32)
    y = xf * lax.rsqrt(jnp.mean(xf * xf, axis=-1, keepdims=True) + EPS)
    return (y * g.astype(jnp.float32)).astype(x.dtype)


def layernorm(x, g):
    xf = x.astype(jnp.float32)
    mu = jnp.mean(xf, axis=-1, keepdims=True)
    var = jnp.mean(jnp.square(xf - mu), axis=-1, keepdims=True)
    y = (xf - mu) * lax.rsqrt(var + EPS)
    return (y * g.astype(jnp.float32)).astype(x.dtype)


def t5_bucket(dist):
    n = jnp.maximum(dist, 0)
    max_exact = REL_BUCKETS // 2
    nf = jnp.maximum(n, 1).astype(jnp.float32)
    large = max_exact + (jnp.log(nf / max_exact) / math.log(REL_MAX_DIST / max_exact)
                         * (REL_BUCKETS - max_exact)).astype(jnp.int32)
    large = jnp.minimum(large, REL_BUCKETS - 1)
    return jnp.where(n < max_exact, n, large)


def spatial_gating_mixer(uv, v_gain, w_s, b_s):
    z = jax.nn.gelu(uv, approximate=False)
    u, v = jnp.split(z, 2, axis=-1)
    v = layernorm(v, v_gain)
    bsz, s, _ = v.shape
    nc = s // A_CHUNK
    v = v.reshape(bsz, nc, A_CHUNK, A_GROUPS, A_GROUP_DIM)
    causal = jnp.tril(jnp.ones((A_CHUNK, A_CHUNK), dtype=w_s.dtype))
    w = w_s * causal[None]
    sv = jnp.einsum('gts,bcsgd->bctgd', w, v) + b_s.T[:, :, None]
    return u * sv.reshape(bsz, s, A_WIDTH)


def moba_mixer(q, k, v, rel_bias):
    bsz, s, _ = q.shape
    L = MOBA_BLOCK
    def heads(t):
        return t.reshape(bsz, s, B_HEADS, B_HEAD_DIM).transpose(0, 2, 1, 3)
    q, k, v = heads(q), heads(k), heads(v)
    nb = -(-s // L)
    pad = nb * L - s
    k = jnp.pad(k, ((0, 0), (0, 0), (0, pad), (0, 0)))
    v = jnp.pad(v, ((0, 0), (0, 0), (0, pad), (0, 0)))
    kb = k.reshape(bsz, B_HEADS, nb, L, B_HEAD_DIM)
    vb = v.reshape(bsz, B_HEADS, nb, L, B_HEAD_DIM)
    kmean = jnp.mean(kb, axis=3)
    topk = min(MOBA_TOPK, nb)
    scale = B_HEAD_DIM ** -0.5
    bidx = jnp.arange(bsz)[:, None, None, None]
    hidx = jnp.arange(B_HEADS)[None, :, None, None]
    bias_hb = rel_bias.T.astype(jnp.float32)
    blk = jnp.arange(nb)
    offs = jnp.arange(L)

    def chunk_fn(c):
        start = c * Q_CHUNK
        qc = lax.dynamic_slice_in_dim(q, start, Q_CHUNK, axis=2)
        qpos = start + jnp.arange(Q_CHUNK)
        own = start // L
        gs = jnp.einsum('bhqd,bhnd->bhqn', qc, kmean).astype(jnp.float32)
        past = blk[None, :] < (qpos // L)[:, None]
        gs = jnp.where(past, gs, NEG)
        top_s, sel = lax.top_k(gs, topk)
        valid = top_s > NEG / 2
        ks = kb[bidx, hidx, sel]
        vs = vb[bidx, hidx, sel]
        kpos_sel = sel[..., None] * L + offs
        b_sel = bias_hb[hidx[..., None], t5_bucket(qpos[:, None, None] - kpos_sel)]
        logit_sel = (jnp.einsum('bhqd,bhqkld->bhqkl', qc, ks) * scale).astype(jnp.float32) + b_sel
        logit_sel = jnp.where(valid[..., None], logit_sel, NEG).reshape(bsz, B_HEADS, Q_CHUNK, topk * L)
        k_own = lax.dynamic_index_in_dim(kb, own, axis=2, keepdims=False)
        v_own = lax.dynamic_index_in_dim(vb, own, axis=2, keepdims=False)
        dist_own = qpos[:, None] - (own * L + offs)[None, :]
        b_own = bias_hb[:, t5_bucket(dist_own)][None]
        logit_own = (jnp.einsum('bhqd,bhld->bhql', qc, k_own) * scale).astype(jnp.float32) + b_own
        logit_own = jnp.where(dist_own >= 0, logit_own, NEG)
        p = jax.nn.softmax(jnp.concatenate([logit_sel, logit_own], axis=-1), axis=-1)
        p_sel = p[..., :topk * L].reshape(bsz, B_HEADS, Q_CHUNK, topk, L).astype(q.dtype)
        p_own = p[..., topk * L:].astype(q.dtype)
        out = (jnp.einsum('bhqkl,bhqkld->bhqd', p_sel, vs)
               + jnp.einsum('bhql,bhld->bhqd', p_own, v_own))
        return out.astype(q.dtype)

    outs = lax.map(chunk_fn, jnp.arange(s // Q_CHUNK, dtype=jnp.int32))
    return outs.transpose(1, 0, 3, 2, 4).reshape(bsz, s, B_WIDTH)


def setup_inputs(seed: int = 0) -> dict:
    key = jax.random.key(seed)
    ks = jax.random.split(key, 16)
    f32 = jnp.float32
    def nrm(k, shape, scale):
        return jax.random.normal(k, shape, f32) * scale
    return {
        "x": nrm(ks[0], (BATCH, SEQ, D_MODEL), 1.0),
        "ln_mix": 1.0 + nrm(ks[1], (DEPTH, D_MODEL), 0.01),
        "w_in": nrm(ks[2], (DEPTH, D_MODEL, IN_COLS), D_MODEL ** -0.5),
        "a_v_gain": 1.0 + nrm(ks[3], (DEPTH, A_WIDTH), 0.01),
        "a_spatial": nrm(ks[4], (DEPTH, A_GROUPS, A_CHUNK, A_CHUNK), A_CHUNK ** -0.5),
        "a_spatial_bias": 1.0 + nrm(ks[5], (DEPTH, A_GROUPS, A_CHUNK), 0.01),
        "w_proj_a": nrm(ks[6], (DEPTH, A_WIDTH, D_MODEL), A_WIDTH ** -0.5),
        "w_proj_b": nrm(ks[7], (DEPTH, B_WIDTH, D_MODEL), B_WIDTH ** -0.5),
        "w_out": nrm(ks[8], (DEPTH, D_MODEL, D_MODEL), D_MODEL ** -0.5),
        "rel_bias": nrm(ks[9], (REL_BUCKETS, B_HEADS), 0.1),
        "ln_mlp": 1.0 + nrm(ks[10], (DEPTH, D_MODEL), 0.01),
        "w_up": nrm(ks[11], (DEPTH, D_MODEL, D_FF), D_MODEL ** -0.5),
        "w_down": nrm(ks[12], (DEPTH, D_FF, D_MODEL), D_FF ** -0.5),
        "ln_final": 1.0 + nrm(ks[13], (D_MODEL,), 0.01),
    }


def reference(x, ln_mix, w_in, a_v_gain, a_spatial, a_spatial_bias, w_proj_a, w_proj_b,
              w_out, rel_bias, ln_mlp, w_up, w_down, ln_final):
    for l in range(DEPTH):
        h = rmsnorm(x, ln_mix[l])
        z = h @ w_in[l]
        uv, qkv, gates = jnp.split(z, [2 * A_WIDTH, 2 * A_WIDTH + 3 * B_WIDTH], axis=-1)
        y_a = spatial_gating_mixer(uv, a_v_gain[l], a_spatial[l], a_spatial_bias[l]) @ w_proj_a[l]
        q, k, v = jnp.split(qkv, 3, axis=-1)
        y_b = moba_mixer(q, k, v, rel_bias) @ w_proj_b[l]
        g_a, g_b = jnp.split(gates, 2, axis=-1)
        merged = jax.nn.sigmoid(g_a) * y_a + jax.nn.sigmoid(g_b) * y_b
        x = x + merged @ w_out[l]
        h = rmsnorm(x, ln_mlp[l])
        x = x + jnp.square(jax.nn.relu(h @ w_up[l])) @ w_down[l]
    return rmsnorm(x, ln_final)
```

```python
import os
import math
import numpy as np
import ml_dtypes
import concourse.bass as bass
import concourse.mybir as mybir
from concourse.bass_utils import run_bass_kernel_spmd

F32 = mybir.dt.float32
BF16 = mybir.dt.bfloat16
AF = mybir.ActivationFunctionType
ALU = mybir.AluOpType
AX = mybir.AxisListType

D = 2048
S = 8192
NH = 16
HD = 128
LB = 256
NBLK = 32
FF = 8192
INC = 14336
TOK = 1024
NPAST = 31 * LB
NVT = S
EPS = 1e-6
SCALE = HD ** -0.5
NEGV = -1.0e5
GROUPS = [(i * 8, 8) for i in range(8)]
RS = [7, 15, 23, 31]
ACT_SHARE = int(os.environ.get('KACT_SHARE', '0'))
PD = int(os.environ.get('KPD', '3'))


class Ev:
    __slots__ = ("key", "sem", "val")

    def __init__(self, key, sem, val):
        self.key = key
        self.sem = sem
        self.val = val


class Res:
    __slots__ = ("name", "w", "r")

    def __init__(self, name):
        self.name = name
        self.w = None
        self.r = {}


class Eng:
    def __init__(self, name, obj):
        self.name = name
        self.obj = obj
        self.sem = None
        self.cnt = 0
        self.epoch = 0
        self.seen = {}
        self.last = None


class K:
    EPOCH = 30000
    NDMA = 16

    def __init__(self, nc):
        self.nc = nc
        self.eng = {
            "pe": Eng("pe", nc.tensor),
            "act": Eng("act", nc.scalar),
            "dve": Eng("dve", nc.vector),
            "pool": Eng("pool", nc.gpsimd),
            "sp": Eng("sp", nc.sync),
        }
        for e in self.eng.values():
            e.sem = nc.alloc_semaphore(f"s_{e.name}_0")
        self.dq = {}
        for qn in ("sp", "pool"):
            self.dq[qn] = {"sem": [nc.alloc_semaphore(f"s_dma_{qn}_{i}") for i in range(self.NDMA)], "cnt": [0] * self.NDMA, "next": 0}
        self.nres = 0

    def res(self, name=None):
        self.nres += 1
        return Res(name or f"r{self.nres}")

    def _wait(self, e, ev):
        if ev is None:
            return
        if e.name == "pe" and ev.key[0] == "pe":
            return
        if e.seen.get(ev.key, 0) >= ev.val:
            return
        e.obj.wait_ge(ev.sem, ev.val)
        e.seen[ev.key] = ev.val

    def _deps(self, e, r, w):
        for x in r:
            self._wait(e, x.w)
        for x in w:
            self._wait(e, x.w)
            for ev in x.r.values():
                self._wait(e, ev)

    def _mark(self, ev, r, w):
        for x in r:
            x.r[ev.key] = ev
        for x in w:
            x.w = ev
            x.r = {}

    def _signal(self, e, ins):
        if e.cnt >= self.EPOCH:
            e.epoch += 1
            e.cnt = 0
            e.sem = self.nc.alloc_semaphore(f"s_{e.name}_{e.epoch}")
        e.cnt += 1
        ins.then_inc(e.sem, 1)
        ev = Ev((e.name, e.epoch), e.sem, e.cnt)
        e.last = ev
        return ev

    def op(self, en, fn, r=(), w=()):
        e = self.eng[en]
        self._deps(e, r, w)
        ins = fn(e.obj)
        ev = self._signal(e, ins)
        self._mark(ev, r, w)
        return ev

    def pe(self, fns, r=(), w=()):
        e = self.eng["pe"]
        self._deps(e, r, w)
        ins = None
        for fn in fns:
            ins = fn(e.obj)
        ev = self._signal(e, ins)
        self._mark(ev, r, w)
        return ev

    def dma(self, qn, out, in_, r=(), w=()):
        e = self.eng[qn]
        q = self.dq[qn]
        self._deps(e, r, w)
        i = q["next"]
        q["next"] = (i + 1) % self.NDMA
        if q["cnt"][i] > 0:
            self._wait(e, Ev(("dma", qn, i), q["sem"][i], q["cnt"][i] * 16))
        q["cnt"][i] += 1
        e.obj.dma_start(out=out, in_=in_).then_inc(q["sem"][i], 16)
        ev = Ev(("dma", qn, i), q["sem"][i], q["cnt"][i] * 16)
        self._mark(ev, r, w)
        return ev

    def _wait_any(self, e, ev):
        if e.seen.get(ev.key, 0) >= ev.val:
            return
        e.obj.wait_ge(ev.sem, ev.val)
        e.seen[ev.key] = ev.val

    def barrier(self):
        evs = [e.last for e in self.eng.values() if e.last is not None]
        for qn, q in self.dq.items():
            for i in range(self.NDMA):
                if q["cnt"][i] > 0:
                    evs.append(Ev(("dma", qn, i), q["sem"][i], q["cnt"][i] * 16))
        for e in self.eng.values():
            for ev in evs:
                self._wait_any(e, ev)


def build(stop_after=None, dbg=False):
    nc = bass.Bass("TRN2", target_bir_lowering=False)
    k = K(nc)

    def din(name, shape, dt=F32):
        return nc.dram_tensor(name, shape, dt, kind="ExternalInput").ap()

    xa = din("xa", [NVT, D])
    xo = din("xo", [TOK, D])
    w_in = din("w_in", [D, INC])
    w_pa = din("w_pa", [D, D])
    w_pb = din("w_pb", [D, D])
    w_out = din("w_out", [D, D])
    w_up = din("w_up", [D, FF])
    w_down = din("w_down", [FF, D])
    g_mix = din("g_mix", [128, D])
    g_v = din("g_v", [128, D])
    g_mlp = din("g_mlp", [128, D])
    g_fin = din("g_fin", [128, D])
    wsT = din("wsT", [128, 16 * 128])
    cmask = din("cmask", [128, 128])
    bsb = din("bsb", [128, 16 * 128])
    tdg = din("tdg", [128, 16 * 128])
    tpv = din("tpv", [128, 16 * 128])
    cfar = din("cfar", [128, 16])
    negm = din("negm", [128, 128])
    pmd = din("pm", [128, 128])
    ohd = din("oh", [128, 128])
    idn = din("idn", [128, 128], BF16)
    out = nc.dram_tensor("out", [TOK, D], F32, kind="ExternalOutput").ap()

    skind = "ExternalOutput" if dbg else "Internal"
    kT_d = nc.dram_tensor("kT_d", [NH, 128, NVT], BF16, kind=skind).ap()
    v_d = nc.dram_tensor("v_d", [NVT, D], BF16, kind=skind).ap()
    qT_d = nc.dram_tensor("qT_d", [NH, 128, TOK], BF16, kind=skind).ap()
    hTo_d = nc.dram_tensor("hTo_d", [128, 16 * TOK], BF16, kind=skind).ap()
    mix_d = nc.dram_tensor("mix_d", [128, 16 * TOK], BF16, kind=skind).ap()
    ksum_d = nc.dram_tensor("ksum_d", [128, 16 * 40], F32, kind=skind).ap()
    att_d = nc.dram_tensor("att_d", [128, 16 * TOK], BF16, kind=skind).ap() if dbg else None
    x2_d = nc.dram_tensor("x2_d", [128, 8 * D], F32, kind=skind).ap() if dbg else None

    w_in_v = w_in.rearrange("(c p) n -> p c n", p=128)
    w_pa_v = w_pa.rearrange("(c p) n -> p c n", p=128)
    w_pb_v = w_pb.rearrange("(c p) n -> p c n", p=128)
    w_out_v = w_out.rearrange("(c p) n -> p c n", p=128)
    w_up_v = w_up.rearrange("(c p) n -> p c n", p=128)

    uid = [0]

    def T(name, shape, dt, off):
        uid[0] += 1
        n = 1
        for s_ in shape[1:]:
            n *= s_
        nbytes = n * (4 if dt == F32 else 2)
        t = nc.alloc_sbuf_tensor_at(f"{name}_{uid[0]}", shape, dt, offset=off)
        return t, off + ((nbytes + 63) // 64) * 64

    ps = [nc.alloc_psum_tensor(f"ps{i}", [128, 512], F32) for i in range(8)]
    r_ps = [k.res(f"ps{i}") for i in range(8)]

    off = 16512
    ident, off = T("ident", [128, 128], BF16, off)
    stat, off = T("stat", [128, 64], F32, off)
    gbc, off = T("gbc", [128, D], F32, off)
    r_ident, r_gbc = k.res(), k.res()
    ksum, off = T("ksum", [128, 16, 40], F32, off)
    r_ksum = k.res()
    P0 = off
    SB_LIMIT = 229280

    k.dma("sp", ident[:], idn[:, :], w=[r_ident])
    k.dma("sp", gbc[:], g_mix[:, :], w=[r_gbc])
    k.op("dve", lambda e: e.memset(ksum[:], 0.0), w=[r_ksum])

    stat_slots = [k.res() for _ in range(8)]
    stat_i = [0]

    def next_stat():
        i = stat_i[0] % 8
        stat_i[0] += 1
        return stat[:, 8 * i:8 * i + 8], stat_slots[i]

    bank_i = [0]

    def next_bank(n=6):
        b = bank_i[0] % n
        bank_i[0] += 1
        return b

    evac_i = [0]

    def evac_copy(dst, src, r, w):
        evac_i[0] += 1
        if evac_i[0] % 2:
            return k.op("act", lambda e: e.activation(out=dst, in_=src, func=AF.Copy), r=r, w=w)
        return k.op("dve", lambda e: e.tensor_copy(out=dst, in_=src), r=r, w=w)

    def rstd_from_ss(ss_ap, out_ap, r_st, n):
        k.op("dve", lambda e: e.tensor_scalar(out=out_ap, in0=ss_ap, scalar1=1.0 / n, scalar2=EPS, op0=ALU.mult, op1=ALU.add), r=[r_st], w=[r_st])
        k.op("act", lambda e: e.activation(out=out_ap, in_=out_ap, func=AF.Sqrt), r=[r_st], w=[r_st])
        k.op("dve", lambda e: e.reciprocal(out=out_ap, in_=out_ap), r=[r_st], w=[r_st])

    def prep_load(src_fn, t, tmp):
        xbuf, r_xb = tmp[0], tmp[1]
        return src_fn(t, xbuf[t % 2], r_xb[t % 2])

    def prep_norm(loaded, t, tmp):
        xbuf, r_xb, xn, r_xn, junk, r_junk = tmp
        b = t % 2
        xin, r_in = loaded
        st, r_st = next_stat()
        k.op("dve", lambda e: e.memset(st[:, 0:2], 0.0), w=[r_st])
        k.op("act", lambda e: e.activation(out=junk[:], in_=xin, func=AF.Square, accum_out=st[:, 0:1]), r=[r_in], w=[r_junk, r_st])
        rstd_from_ss(st[:, 0:1], st[:, 1:2], r_st, D)
        k.op("dve", lambda e: e.scalar_tensor_tensor(out=xn[b][:], in0=xin, scalar=st[:, 1:2], in1=gbc[:], op0=ALU.mult, op1=ALU.mult),
             r=[r_in, r_st, r_gbc], w=[r_xn[b]])

    def prep_tr(t, hT, r_hT_tiles, tmp):
        xbuf, r_xb, xn, r_xn, junk, r_junk = tmp
        b = t % 2
        for half in range(2):
            pb = ps[6 + half][:].bitcast(BF16)
            k.pe([(lambda e, c=c: e.transpose(out=pb[:, (c % 8) * 128:(c % 8 + 1) * 128], in_=xn[b][:, c * 128:(c + 1) * 128], identity=ident[:]))
                  for c in range(half * 8, half * 8 + 8)], r=[r_xn[b], r_ident], w=[r_ps[6 + half]])
            src = pb.rearrange("p (c t) -> p c t", c=8)
            dst = hT[:, half * 8:half * 8 + 8, t * 128:(t + 1) * 128]
            if half == 0:
                k.op("act", lambda e: e.activation(out=dst, in_=src, func=AF.Copy), r=[r_ps[6]], w=[r_hT_tiles[t]])
            else:
                k.op("dve", lambda e: e.tensor_copy(out=dst, in_=src), r=[r_ps[7]], w=[r_hT_tiles[t]])

    def prep_tile(src_fn, t, hT, r_hT_tiles, tmp):
        ld = prep_load(src_fn, t, tmp)
        prep_norm(ld, t, tmp)
        prep_tr(t, hT, r_hT_tiles, tmp)

    def prep_hT(src_fn, ntiles, hT, r_hT_tiles, tmp):
        for t in range(ntiles):
            prep_tile(src_fn, t, hT, r_hT_tiles, tmp)

    def mm_fm(bank, N, W, wc0, act, a0, r, nchunk=16):
        k.pe([(lambda e, c=c: e.matmul(ps[bank][:, 0:N], lhsT=W[:, c, wc0:wc0 + 128], rhs=act[:, c, a0:a0 + N], start=(c == 0), stop=(c == nchunk - 1)))
              for c in range(nchunk)], r=r, w=[r_ps[bank]])

    def mm_tm(bank, act, a0, W, N, r, nchunk=16):
        k.pe([(lambda e, c=c: e.matmul(ps[bank][:, 0:N], lhsT=act[:, c, a0:a0 + 128], rhs=W[:, c, 0:N], start=(c == 0), stop=(c == nchunk - 1)))
              for c in range(nchunk)], r=r, w=[r_ps[bank]])

    slab_i = [0]

    def load_slab(slabs, r_slabs, view):
        b = slab_i[0] % 2
        slab_i[0] += 1
        k.dma("pool", slabs[b][:], view, w=[r_slabs[b]])
        return slabs[b], r_slabs[b]

    off = P0
    hTb = [None, None]
    hTb[0], off = T("hT0", [128, 16, TOK], BF16, off)
    hTb[1], off = T("hT1", [128, 16, TOK], BF16, off)
    r_hT = [[k.res() for _ in range(8)] for _ in range(2)]
    xbuf = [None, None]
    xbuf[0], off = T("xb0", [128, D], F32, off)
    xbuf[1], off = T("xb1", [128, D], F32, off)
    r_xb = [k.res(), k.res()]
    xn = [None, None]
    xn[0], off = T("xn0", [128, D], BF16, off)
    xn[1], off = T("xn1", [128, D], BF16, off)
    r_xn = [k.res(), k.res()]
    junk, off = T("junk", [128, D], BF16, off)
    r_junk = k.res()
    tmpA = (xbuf, r_xb, xn, r_xn, junk, r_junk)
    slabs = [None, None]
    slabs[0], off = T("slab0", [128, 16, 512], BF16, off)
    slabs[1], off = T("slab1", [128, 16, 512], BF16, off)
    r_slabs = [k.res(), k.res()]
    offA = off
    kst = [None, None]
    kst[0], off = T("kst0", [128, 4, TOK], BF16, off)
    kst[1], off = T("kst1", [128, 4, TOK], BF16, off)
    r_kst = [k.res(), k.res()]
    vst = [None, None]
    vst[0], off = T("vst0", [128, 8, 512], BF16, off)
    vst[1], off = T("vst1", [128, 8, 512], BF16, off)
    r_vst = [k.res(), k.res()]
    assert off <= SB_LIMIT, off

    kT_dv = kT_d.rearrange("h d t -> d h t")
    st_i = 0
    def src_a_of(t0):
        def src_a(t, xb, rxb):
            k.dma("sp", xb[:], xa[(t0 + t) * 128:(t0 + t + 1) * 128, :], w=[rxb])
            return xb[:], rxb
        return src_a

    def src_o(t, xb, rxb):
        k.dma("sp", xb[:], xo[t * 128:(t + 1) * 128, :], w=[rxb])
        return xb[:], rxb

    NG = len(GROUPS)
    prep_hT(src_a_of(0), GROUPS[0][1], hTb[NG % 2], r_hT[NG % 2], tmpA)
    for gi, (t0, nt) in enumerate(GROUPS):
        hT = hTb[(gi + NG) % 2]
        rh = r_hT[(gi + NG) % 2]
        NT = nt * 128
        if gi + 1 < len(GROUPS):
            nxt = (src_a_of(GROUPS[gi + 1][0]), GROUPS[gi + 1][1], hTb[(gi + 1 + NG) % 2], r_hT[(gi + 1 + NG) % 2])
        else:
            nxt = (src_o, 8, hTb[(gi + 1 + NG) % 2], r_hT[(gi + 1 + NG) % 2])
        if gi == 0:
            pending = prep_load(nxt[0], 0, tmpA)
        for sl in range(8):
            do_prep = sl < nxt[1]
            if do_prep:
                prep_norm(pending, sl, tmpA)
            if sl + 1 < nxt[1]:
                pending = prep_load(nxt[0], sl + 1, tmpA)
            slab, r_sl = load_slab(slabs, r_slabs, w_in_v[:, :, 6144 + sl * 512:6144 + (sl + 1) * 512])
            sb = st_i % 2
            st_i += 1
            if sl < 4:
                for hh in range(4):
                    for a0 in range(0, NT, 512):
                        N = min(512, NT - a0)
                        bk = next_bank()
                        mm_fm(bk, N, slab, hh * 128, hT, a0, r=[r_sl] + rh[a0 // 128:(a0 + N) // 128])
                        for bo in range(0, N, LB):
                            vb = (t0 * 128 + a0 + bo) // LB
                            k.op("act", lambda e, bo=bo, vb=vb: e.activation(out=kst[sb][:, hh, a0 + bo:a0 + bo + LB], in_=ps[bk][:, bo:bo + LB], func=AF.Copy,
                                                                             accum_out=ksum[:, sl * 4 + hh, vb:vb + 1]),
                                 r=[r_ps[bk]], w=[r_kst[sb], r_ksum])
                k.dma("sp", kT_dv[:, sl * 4:(sl + 1) * 4, t0 * 128:t0 * 128 + NT], kst[sb][:, :, 0:NT], r=[r_kst[sb]])
            else:
                for t in range(nt):
                    bk = next_bank()
                    mm_tm(bk, hT, t * 128, slab, 512, r=[r_sl, rh[t]])
                    k.op("dve", lambda e: e.tensor_copy(out=vst[sb][:, t, :], in_=ps[bk][:, :]), r=[r_ps[bk]], w=[r_vst[sb]])
                k.dma("sp", v_d[t0 * 128:t0 * 128 + NT, (sl - 4) * 512:(sl - 3) * 512].rearrange("(n p) c -> p n c", p=128),
                      vst[sb][:, 0:nt, :], r=[r_vst[sb]])
            if do_prep:
                prep_tr(sl, nxt[2], nxt[3], tmpA)
            if sl == 7 and gi + 2 < len(GROUPS) + 1:
                if gi + 2 < len(GROUPS):
                    pending = prep_load(src_a_of(GROUPS[gi + 2][0]), 0, tmpA)
                else:
                    pending = prep_load(src_o, 0, tmpA)
    OWN_BUF = (2 * NG) % 2
    k.dma("sp", ksum_d[:, :], ksum[:].rearrange("p h n -> p (h n)"), r=[r_ksum])

    if stop_after == "A":
        k.barrier()
        return nc

    k.barrier()
    assert OWN_BUF == 0
    hTo = hTb[0]
    rho = r_hT[0]
    off = P0 + 32 * 1024
    vg, off = T("vg", [128, 8, D], F32, off)
    r_vg = [k.res() for _ in range(8)]
    offB1 = off
    vln, off = T("vln", [128, 8, D], BF16, off)
    r_vln = [k.res() for _ in range(8)]
    slabs = [None, None]
    slabs[0], off = T("slabB0", [128, 16, 512], BF16, off)
    slabs[1], off = T("slabB1", [128, 16, 512], BF16, off)
    r_slabs = [k.res(), k.res()]
    vgbc, off = T("vgbc", [128, D], F32, off)
    r_vgbc = k.res()
    wTm, off = T("wTm", [128, 16, 128], BF16, off)
    r_wTm = k.res()
    bbc, off = T("bbc", [128, 16 * 128], F32, off)
    r_bbc = k.res()
    st6, off = T("st6", [128, 4, 6], F32, off)
    r_st6 = k.res()
    qst = [None, None]
    qst[0], off = T("qst0", [128, 4, TOK], BF16, off)
    qst[1], off = T("qst1", [128, 4, TOK], BF16, off)
    r_qst = [k.res(), k.res()]
    assert off <= SB_LIMIT, off
    k.dma("sp", vgbc[:], g_v[:, :], w=[r_vgbc])
    k.dma("sp", bbc[:], bsb[:, :], w=[r_bbc])
    wtmp = vg[:, 0, :]
    k.dma("sp", wtmp, wsT[:, :], w=[r_vg[0]])
    cm = vg[:, 1, 0:128]
    k.dma("sp", cm, cmask[:, :], w=[r_vg[1]])
    k.op("dve", lambda e: e.tensor_tensor(out=wTm[:], in0=wtmp.rearrange("p (g t) -> p g t", g=16),
                                          in1=cm.unsqueeze(1).broadcast_to([128, 16, 128]), op=ALU.mult),
         r=[r_vg[0], r_vg[1]], w=[r_wTm])

    for sl in range(4):
        slab, r_sl = load_slab(slabs, r_slabs, w_in_v[:, :, 2048 + sl * 512:2048 + (sl + 1) * 512])
        for t in range(8):
            bk = next_bank()
            mm_tm(bk, hTo, t * 128, slab, 512, r=[r_sl, rho[t]])
            k.op("act", lambda e: e.activation(out=vg[:, t, sl * 512:(sl + 1) * 512], in_=ps[bk][:, :], func=AF.Gelu), r=[r_ps[bk]], w=[r_vg[t]])
    qT_dv = qT_d.rearrange("h d t -> d h t")

    def q_chunk(i):
        sl, hp = i // 2, i % 2
        sb = sl % 2
        if hp == 0:
            q_chunk.cur = load_slab(slabs, r_slabs, w_in_v[:, :, 4096 + sl * 512:4096 + (sl + 1) * 512])
        slab, r_sl = q_chunk.cur
        for hh in (2 * hp, 2 * hp + 1):
            for half in range(2):
                bk = next_bank()
                mm_fm(bk, 512, slab, hh * 128, hTo, half * 512, r=[r_sl] + rho[half * 4:half * 4 + 4])
                k.op("act", lambda e: e.activation(out=qst[sb][:, hh, half * 512:(half + 1) * 512], in_=ps[bk][:, :], func=AF.Copy), r=[r_ps[bk]], w=[r_qst[sb]])
        if hp == 1:
            k.dma("sp", qT_dv[:, sl * 4:(sl + 1) * 4, :], qst[sb][:], r=[r_qst[sb]])

    for t in range(8):
        st, r_st = next_stat()
        for c in range(4):
            k.op("dve", lambda e, c=c: e.bn_stats(out=st6[:, c, :], in_=vg[:, t, c * 512:(c + 1) * 512]), r=[r_vg[t]], w=[r_st6])
        k.op("dve", lambda e: e.bn_aggr(out=st[:, 0:2], in_=st6[:]), r=[r_st6], w=[r_st])
        rstd_from_ss(st[:, 1:2], st[:, 2:3], r_st, 1.0)
        k.op("dve", lambda e: e.tensor_scalar(out=vg[:, t, :], in0=vg[:, t, :], scalar1=st[:, 0:1], scalar2=st[:, 2:3], op0=ALU.subtract, op1=ALU.mult),
             r=[r_st], w=[r_vg[t]])
        k.op("dve", lambda e: e.tensor_tensor(out=vln[:, t, :], in0=vg[:, t, :], in1=vgbc[:], op=ALU.mult), r=[r_vg[t], r_vgbc], w=[r_vln[t]])
        q_chunk(t)
    k.barrier()
    off = P0 + 32 * 1024
    mixT, off = T("mixT", [128, 16, TOK], BF16, off)
    r_mix = k.res()
    gu = [None, None]
    gu[0], off = T("gu0", [128, 512], F32, off)
    gu[1], off = T("gu1", [128, 512], F32, off)
    r_gu = [k.res(), k.res()]
    t2 = [None, None]
    t2[0], off = T("t20", [128, 512], F32, off)
    t2[1], off = T("t21", [128, 512], F32, off)
    r_t2 = [k.res(), k.res()]
    assert off <= offB1, (off, offB1)
    ui = 0
    for g in range(16):
        if g % 4 == 0:
            slab, r_sl = load_slab(slabs, r_slabs, w_in_v[:, :, (g // 4) * 512:(g // 4 + 1) * 512])
        for half in range(2):
            bu = next_bank()
            mm_fm(bu, 512, slab, (g % 4) * 128, hTo, half * 512, r=[r_sl] + rho[half * 4:half * 4 + 4])
            bs = next_bank()
            k.pe([(lambda e, t=t: e.matmul(ps[bs][:, (t % 4) * 128:(t % 4 + 1) * 128], lhsT=vln[:, t, g * 128:(g + 1) * 128], rhs=wTm[:, g, :], start=True, stop=True))
                  for t in range(half * 4, half * 4 + 4)], r=[r_vln[t] for t in range(half * 4, half * 4 + 4)] + [r_wTm], w=[r_ps[bs]])
            i = ui % 2
            ui += 1
            k.op("act", lambda e: e.activation(out=gu[i][:], in_=ps[bu][:, :], func=AF.Gelu), r=[r_ps[bu]], w=[r_gu[i]])
            k.op("dve", lambda e: e.tensor_tensor(out=t2[i][:].rearrange("p (a b) -> p a b", a=4), in0=ps[bs][:, :].rearrange("p (a b) -> p a b", a=4),
                                                  in1=bbc[:, g * 128:(g + 1) * 128].unsqueeze(1).broadcast_to([128, 4, 128]), op=ALU.add),
                 r=[r_ps[bs], r_bbc], w=[r_t2[i]])
            k.op("dve", lambda e: e.tensor_tensor(out=mixT[:, g, half * 512:(half + 1) * 512], in0=t2[i][:], in1=gu[i][:], op=ALU.mult),
                 r=[r_t2[i], r_gu[i]], w=[r_mix])
    k.dma("sp", hTo_d[:, :], hTo[:].rearrange("p c t -> p (c t)"), r=rho)
    k.dma("sp", mix_d[:, :], mixT[:].rearrange("p c t -> p (c t)"), r=[r_mix])
    if stop_after == "B":
        k.barrier()
        return nc

    k.barrier()
    off = P0
    attnT, off = T("attnT", [128, 16, TOK], BF16, off)
    r_att = k.res()
    offC0 = off
    BdT, off = T("BdT", [128, 16, 128], BF16, off)
    BpT, off = T("BpT", [128, 16, 128], BF16, off)
    Ecr, off = T("Ecr", [128, 16, 128], BF16, off)
    r_bias = k.res()
    pm_sb, off = T("pm_sb", [128, 128], F32, off)
    oh_sb, off = T("oh_sb", [128, 128], F32, off)
    r_pm = k.res()
    kTb, vab, qTb = [None, None], [None, None], [None, None]
    r_kv = [k.res(), k.res()]
    for b in range(2):
        kTb[b], off = T(f"kT{b}", [128, NVT], BF16, off)
        vab[b], off = T(f"va{b}", [128, NVT // 128, 130], BF16, off)
        qTb[b], off = T(f"qT{b}", [128, TOK], BF16, off)
    km, kmb, kml, gsm, sel, m8, selp, stmp = [[None, None] for _ in range(8)]
    r_km = [k.res(), k.res()]
    r_sel = [k.res(), k.res()]
    for b in range(2):
        km[b], off = T(f"km{b}", [128, 32], F32, off)
        kmb[b], off = T(f"kmb{b}", [128, 32], BF16, off)
        kml[b], off = T(f"kml{b}", [128, 32], BF16, off)
        gsm[b], off = T(f"gsm{b}", [128, 8, 32], F32, off)
        sel[b], off = T(f"sel{b}", [128, 8, 32], F32, off)
        m8[b], off = T(f"m8{b}", [128, 8, 8], F32, off)
        selp[b], off = T(f"selp{b}", [128, 8], F32, off)
        stmp[b], off = T(f"stmp{b}", [128, 32], F32, off)
    tma = [None] * 4
    r_tma = [k.res() for _ in range(4)]
    for i_ in range(4):
        tma[i_], off = T(f"tma{i_}", [128, 2, 132], F32, off)
    acc, off = T("acc", [128, 8, 132], F32, off)
    r_acc = [k.res() for _ in range(8)]
    PT = [None] * 4
    for i_ in range(4):
        PT[i_], off = T(f"PT{i_}", [128, 512], BF16, off)
    r_PT = [k.res() for _ in range(4)]
    PTl, off = T("PTl", [128, 512], BF16, off)
    r_PTl = k.res()
    obf, off = T("obf", [128, 8, 128], BF16, off)
    r_obf = k.res()
    rec, off = T("rec", [128, 8], F32, off)
    r_rec = k.res()
    tdg_sb, off2 = T("tdg_sb", [128, 16, 128], F32, off)
    tpv_sb, off2 = T("tpv_sb", [128, 16, 128], F32, off2)
    cf_sb, off2 = T("cf_sb", [128, 16], F32, off2)
    ng_sb, off2 = T("ng_sb", [128, 128], F32, off2)
    r_tb = k.res()
    assert off2 <= SB_LIMIT, off2

    k.dma("sp", tdg_sb[:].rearrange("p h q -> p (h q)"), tdg[:, :], w=[r_tb])
    k.dma("sp", tpv_sb[:].rearrange("p h q -> p (h q)"), tpv[:, :], w=[r_tb])
    k.dma("sp", cf_sb[:], cfar[:, :], w=[r_tb])
    k.dma("sp", ng_sb[:], negm[:, :], w=[r_tb])
    k.dma("sp", pm_sb[:], pmd[:, :], w=[r_pm])
    k.dma("sp", oh_sb[:], ohd[:, :], w=[r_pm])
    k.dma("sp", ksum[:].rearrange("p h n -> p (h n)"), ksum_d[:, :], w=[r_ksum])
    for h in range(NH):
        k.op("dve", lambda e, h=h: e.tensor_scalar(out=tdg_sb[:, h, :], in0=tdg_sb[:, h, :], scalar1=cf_sb[:, h:h + 1], scalar2=1.0 / SCALE, op0=ALU.subtract, op1=ALU.mult),
             r=[r_tb], w=[r_tb])
        k.op("dve", lambda e, h=h: e.tensor_scalar(out=tpv_sb[:, h, :], in0=tpv_sb[:, h, :], scalar1=cf_sb[:, h:h + 1], scalar2=1.0 / SCALE, op0=ALU.subtract, op1=ALU.mult),
             r=[r_tb], w=[r_tb])
    k.op("dve", lambda e: e.tensor_tensor(out=BdT[:], in0=tdg_sb[:], in1=ng_sb[:].unsqueeze(1).broadcast_to([128, 16, 128]), op=ALU.add), r=[r_tb], w=[r_bias])
    k.op("dve", lambda e: e.tensor_copy(out=BpT[:], in_=tpv_sb[:]), r=[r_tb], w=[r_bias])
    k.op("act", lambda e: e.activation(out=tpv_sb[:], in_=tpv_sb[:], func=AF.Exp, scale=SCALE), r=[r_tb], w=[r_tb])
    k.op("dve", lambda e: e.tensor_scalar(out=Ecr[:], in0=tpv_sb[:], scalar1=-1.0, scalar2=None, op0=ALU.add), r=[r_tb], w=[r_bias])
    for b in range(2):
        k.op("dve", lambda e, b=b: e.memset(vab[b][:, :, 128:130], 1.0), w=[r_kv[b]])
    for b in range(2):
        k.op("dve", lambda e, b=b: e.memset(km[b][:], 0.0), w=[r_km[b]])

    def head_loads(h):
        b = h % 2
        k.dma("sp", kTb[b][:], kT_d[h, :, :], w=[r_kv[b]])
        k.dma("sp", vab[b][:, :, 0:128], v_d[:, h * 128:(h + 1) * 128].rearrange("(n p) d -> p n d", p=128), w=[r_kv[b]])
        k.dma("sp", qTb[b][:], qT_d[h, :, :], w=[r_kv[b]])

    def gating(h):
        b = h % 2
        kT, qT, rkv = kTb[b], qTb[b], r_kv[b]
        kmh = ksum[:, h, 0:32]
        k.op("dve", lambda e: e.tensor_scalar(out=kmb[b][:], in0=kmh, scalar1=1.0 / LB, scalar2=None, op0=ALU.mult), r=[r_ksum], w=[r_km[b]])
        k.op("dve", lambda e: e.scalar_tensor_tensor(out=kml[b][:], in0=kmh, scalar=1.0 / LB, in1=kmb[b][:], op0=ALU.mult, op1=ALU.subtract), r=[r_ksum, r_km[b]], w=[r_km[b]])
        fns = []
        for t in range(8):
            fns.append(lambda e, t=t: e.matmul(ps[7][:, 256 + t * 32:256 + (t + 1) * 32], lhsT=qT[:, t * 128:(t + 1) * 128], rhs=kmb[b][:, 0:32], start=True, stop=False))
            fns.append(lambda e, t=t: e.matmul(ps[7][:, 256 + t * 32:256 + (t + 1) * 32], lhsT=qT[:, t * 128:(t + 1) * 128], rhs=kml[b][:, 0:32], start=False, stop=True))
        k.pe(fns, r=[rkv, r_km[b]], w=[r_ps[7]])
        k.op("dve", lambda e: e.tensor_tensor(out=gsm[b][:].rearrange("p (s t) n -> p s t n", s=4), in0=ps[7][:, 256:512].rearrange("p (s t n) -> p s t n", s=4, t=2),
                                              in1=pm_sb[:].rearrange("p (s n) -> p s n", s=4).unsqueeze(2).broadcast_to([128, 4, 2, 32]), op=ALU.add),
             r=[r_ps[7], r_pm], w=[r_sel[b]])
        for t in range(8):
            k.op("dve", lambda e, t=t: e.max(out=m8[b][:, t, :], in_=gsm[b][:, t, :]), r=[r_sel[b]], w=[r_sel[b]])
        k.op("dve", lambda e: e.tensor_scalar(out=m8[b][:, :, 2:3], in0=m8[b][:, :, 2:3], scalar1=-1.0e29, scalar2=None, op0=ALU.max), r=[r_sel[b]], w=[r_sel[b]])
        k.op("dve", lambda e: e.tensor_tensor(out=sel[b][:], in0=gsm[b][:], in1=m8[b][:, :, 2:3].broadcast_to([128, 8, 32]), op=ALU.is_ge), r=[r_sel[b]], w=[r_sel[b]])
        for s_ in range(4):
            k.op("dve", lambda e, s_=s_: e.tensor_tensor(out=stmp[b][:], in0=sel[b][:, 2 * s_, :], in1=oh_sb[:, s_ * 32:(s_ + 1) * 32], op=ALU.mult), r=[r_sel[b], r_pm], w=[r_sel[b]])
            k.op("dve", lambda e, s_=s_: e.tensor_reduce(out=selp[b][:, s_:s_ + 1], in_=stmp[b][:], axis=AX.X, op=ALU.add), r=[r_sel[b]], w=[r_sel[b]])

    head_loads(0)
    gating(0)
    deferred = [None]
    sT_i = [0]
    tma_i = [0]
    for h in range(NH):
        b = h % 2
        kT, va, qT = kTb[b], vab[b], qTb[b]
        rkv = r_kv[b]
        selh, selph, r_selh = sel[b], selp[b], r_sel[b]
        if h + 1 < NH:
            head_loads(h + 1)
        k.op("pool", lambda e: e.memset(acc[:], 0.0), w=r_acc)

        blocks_ = [(s_, j) for s_ in range(4) for j in range(RS[s_])]
        SB = [0, 1, 6]

        def emit_qk(blk):
            s_, j = blk
            i_ = sT_i[0]
            sT_i[0] += 1
            sb_ = SB[i_ % 3]
            pt_ = i_ % 4
            k.pe([(lambda e, kt=kt: e.matmul(ps[sb_][:, kt * 256:(kt + 1) * 256], lhsT=kT[:, (2 * j + kt) * 128:(2 * j + kt + 1) * 128],
                                             rhs=qT[:, s_ * 256:(s_ + 1) * 256], start=True, stop=True)) for kt in range(2)],
                 r=[rkv], w=[r_ps[sb_]])
            k.op("act", lambda e: e.activation(out=PT[pt_][:], in_=ps[sb_][:, :], func=AF.Exp, scale=SCALE), r=[r_ps[sb_]], w=[r_PT[pt_]])
            return pt_

        def emit_pv(blk, pt_, n_):
            s_, j = blk
            ob = 2 + (n_ % 4)
            fns = []
            for qt in range(2):
                for kt in range(2):
                    fns.append(lambda e, qt=qt, kt=kt: e.matmul(ps[ob][:, qt * 256:qt * 256 + 129], lhsT=PT[pt_][:, kt * 256 + qt * 128:kt * 256 + (qt + 1) * 128],
                                                                rhs=va[:, 2 * j + kt, 0:129], start=(kt == 0), stop=(kt == 1)))
            k.pe(fns, r=[r_PT[pt_], rkv], w=[r_ps[ob]])
            use_act = ACT_SHARE > 0 and (n_ % ACT_SHARE) == ACT_SHARE - 1
            for qt in range(2):
                tl = s_ * 2 + qt
                src_ = ps[ob][:, qt * 256:qt * 256 + 129]
                if use_act:
                    x_ = tma_i[0] % 4
                    tma_i[0] += 1
                    k.op("act", lambda e: e.activation(out=tma[x_][:, 0, 0:129], in_=src_, func=AF.Copy, scale=selh[:, tl, j:j + 1]),
                         r=[r_ps[ob], r_selh], w=[r_tma[x_]])
                    k.op("pool", lambda e: e.tensor_tensor(out=acc[:, tl, 0:129], in0=acc[:, tl, 0:129], in1=tma[x_][:, 0, 0:129], op=ALU.add),
                         r=[r_tma[x_]], w=[r_acc[tl]])
                else:
                    k.op("dve", lambda e: e.scalar_tensor_tensor(out=acc[:, tl, 0:129], in0=src_, scalar=selh[:, tl, j:j + 1], in1=acc[:, tl, 0:129], op0=ALU.mult, op1=ALU.add),
                         r=[r_ps[ob], r_selh], w=[r_acc[tl]])

        def emit_qk_local(s):
            LTI = [16 * s + 13, 16 * s + 14, 16 * s + 15]
            Q0, Q1 = 2 * s, 2 * s + 1
            i_ = sT_i[0]
            sT_i[0] += 1
            sb_ = SB[i_ % 3]
            pt_ = i_ % 4

            def kt_(i):
                return kT[:, LTI[i] * 128:(LTI[i] + 1) * 128]

            def q_(i):
                return qT[:, i * 128:(i + 1) * 128]

            k.pe([lambda e: e.matmul(ps[sb_][:, 0:128], lhsT=kt_(1), rhs=q_(Q0), start=True, stop=False),
                  lambda e: e.matmul(ps[sb_][:, 0:128], lhsT=ident[:], rhs=BdT[:, h, :], start=False, stop=True),
                  lambda e: e.matmul(ps[sb_][:, 128:256], lhsT=kt_(1), rhs=q_(Q1), start=True, stop=False),
                  lambda e: e.matmul(ps[sb_][:, 128:256], lhsT=ident[:], rhs=BpT[:, h, :], start=False, stop=True),
                  lambda e: e.matmul(ps[sb_][:, 256:384], lhsT=kt_(2), rhs=q_(Q1), start=True, stop=False),
                  lambda e: e.matmul(ps[sb_][:, 256:384], lhsT=ident[:], rhs=BdT[:, h, :], start=False, stop=True),
                  lambda e: e.matmul(ps[sb_][:, 384:512], lhsT=kt_(0), rhs=q_(Q0), start=True, stop=True)],
                 r=[rkv, r_ident, r_bias], w=[r_ps[sb_]])
            k.op("act", lambda e: e.activation(out=PT[pt_][:], in_=ps[sb_][:, :], func=AF.Exp, scale=SCALE), r=[r_ps[sb_]], w=[r_PT[pt_]])
            k.op("dve", lambda e: e.tensor_tensor(out=PT[pt_][:, 384:512], in0=PT[pt_][:, 384:512], in1=Ecr[:, h, :], op=ALU.mult), r=[r_bias], w=[r_PT[pt_]])
            return pt_

        def emit_pv_local(s, pt_, n_):
            LTI = [16 * s + 13, 16 * s + 14, 16 * s + 15]
            Q0, Q1 = 2 * s, 2 * s + 1
            ob, ob2 = 2 + (n_ % 4), 2 + ((n_ + 1) % 4)
            PTl = PT[pt_]
            k.pe([lambda e: e.matmul(ps[ob][:, 0:129], lhsT=PTl[:, 0:128], rhs=va[:, LTI[1], 0:129], start=True, stop=True),
                  lambda e: e.matmul(ps[ob][:, 256:385], lhsT=PTl[:, 128:256], rhs=va[:, LTI[1], 0:129], start=True, stop=False),
                  lambda e: e.matmul(ps[ob][:, 256:385], lhsT=PTl[:, 256:384], rhs=va[:, LTI[2], 0:129], start=False, stop=True),
                  lambda e: e.matmul(ps[ob2][:, 0:129], lhsT=PTl[:, 384:512], rhs=va[:, LTI[0], 0:129], start=True, stop=True)],
                 r=[r_PT[pt_], rkv], w=[r_ps[ob], r_ps[ob2]])
            k.op("dve", lambda e: e.tensor_tensor(out=acc[:, Q0, 0:129], in0=acc[:, Q0, 0:129], in1=ps[ob][:, 0:129], op=ALU.add), r=[r_ps[ob]], w=[r_acc[Q0]])
            k.op("dve", lambda e: e.tensor_tensor(out=acc[:, Q1, 0:129], in0=acc[:, Q1, 0:129], in1=ps[ob][:, 256:385], op=ALU.add), r=[r_ps[ob]], w=[r_acc[Q1]])
            k.op("dve", lambda e: e.scalar_tensor_tensor(out=acc[:, Q0, 0:129], in0=ps[ob2][:, 0:129], scalar=selph[:, s:s + 1], in1=acc[:, Q0, 0:129], op0=ALU.mult, op1=ALU.add),
                 r=[r_ps[ob2], r_selh], w=[r_acc[Q0]])

        items = [("l", s_) for s_ in range(4)] + [("p", blk) for blk in blocks_]
        pend = []
        n_ = 0
        for ii, (kind, arg) in enumerate(items):
            if kind == "p":
                pend.append((kind, arg, emit_qk(arg), n_))
                n_ += 1
            else:
                pend.append((kind, arg, emit_qk_local(arg), n_))
                n_ += 2
            if len(pend) > PD:
                kd, a_, pt_, nn = pend.pop(0)
                (emit_pv if kd == "p" else emit_pv_local)(a_, pt_, nn)
            if ii == 3 and deferred[0] is not None:
                deferred[0]()
                deferred[0] = None
            if ii == 12 and h + 1 < NH:
                gating(h + 1)
        while pend:
            kd, a_, pt_, nn = pend.pop(0)
            (emit_pv if kd == "p" else emit_pv_local)(a_, pt_, nn)

        k.op("dve", lambda e: e.reciprocal(out=rec[:], in_=acc[:, :, 128]), r=r_acc, w=[r_rec])
        k.op("dve", lambda e: e.tensor_tensor(out=obf[:], in0=acc[:, :, 0:128], in1=rec[:].unsqueeze(2).broadcast_to([128, 8, 128]), op=ALU.mult), r=r_acc + [r_rec], w=[r_obf])

        def fin(h=h):
            pb = ps[7][:].bitcast(BF16)
            for hf in range(2):
                k.pe([(lambda e, t=t: e.transpose(out=pb[:, (t % 4) * 128:(t % 4 + 1) * 128], in_=obf[:, t, :], identity=ident[:])) for t in range(hf * 4, hf * 4 + 4)],
                     r=[r_obf, r_ident], w=[r_ps[7]])
                k.op("act", lambda e: e.activation(out=attnT[:, h, hf * 512:(hf + 1) * 512], in_=pb[:, 0:512], func=AF.Copy), r=[r_ps[7]], w=[r_att])

        deferred[0] = fin
    deferred[0]()
    if dbg:
        k.dma("sp", att_d[:, :], attnT[:].rearrange("p c t -> p (c t)"), r=[r_att])
    if stop_after == "C":
        k.barrier()
        return nc

    k.barrier()
    off = offC0
    mergedT, off = T("mergedT", [128, 16, TOK], BF16, off)
    r_mer = k.res()
    offD1 = off
    hTo, off = T("hToD", [128, 16, TOK], BF16, off)
    mixT, off = T("mixTD", [128, 16, TOK], BF16, off)
    r_ho, r_mi = k.res(), k.res()
    wu = [None, None]
    wu[0], off = T("wu0", [128, 16, 4, 256], BF16, off)
    wu[1], off = T("wu1", [128, 16, 4, 256], BF16, off)
    r_wu = [k.res(), k.res()]
    sa, off = T("sa", [128, 512], F32, off)
    sbb, off = T("sb", [128, 512], F32, off)
    r_sa, r_sb = k.res(), k.res()
    assert off <= SB_LIMIT, off
    k.dma("sp", hTo[:].rearrange("p c t -> p (c t)"), hTo_d[:, :], w=[r_ho])
    k.dma("sp", mixT[:].rearrange("p c t -> p (c t)"), mix_d[:, :], w=[r_mi])
    it = 0
    for jj in range(8):
        b = jj % 2
        for i, view in enumerate((w_pa_v[:, :, jj * 256:(jj + 1) * 256], w_pb_v[:, :, jj * 256:(jj + 1) * 256],
                                  w_in_v[:, :, 10240 + jj * 256:10240 + (jj + 1) * 256], w_in_v[:, :, 12288 + jj * 256:12288 + (jj + 1) * 256])):
            k.dma("pool", wu[b][:, :, i, :], view, w=[r_wu[b]])
        for jh in range(2):
            j = jj * 2 + jh
            for half in range(2):
                bs = 4 * (it % 2)
                it += 1
                a0 = half * 512
                for i, (act_, r_act) in enumerate(((mixT, r_mi), (attnT, r_att), (hTo, r_ho), (hTo, r_ho))):
                    k.pe([(lambda e, c=c, i=i, act_=act_: e.matmul(ps[bs + i][:, 0:512], lhsT=wu[b][:, c, i, jh * 128:(jh + 1) * 128], rhs=act_[:, c, a0:a0 + 512], start=(c == 0), stop=(c == 15)))
                          for c in range(16)], r=[r_wu[b], r_act], w=[r_ps[bs + i]])
                k.op("act", lambda e: e.activation(out=sa[:], in_=ps[bs + 2][:, :], func=AF.Sigmoid), r=[r_ps[bs + 2]], w=[r_sa])
                k.op("act", lambda e: e.activation(out=sbb[:], in_=ps[bs + 3][:, :], func=AF.Sigmoid), r=[r_ps[bs + 3]], w=[r_sb])
                k.op("dve", lambda e: e.tensor_tensor(out=sa[:], in0=sa[:], in1=ps[bs + 0][:, :], op=ALU.mult), r=[r_ps[bs + 0]], w=[r_sa])
                k.op("dve", lambda e: e.tensor_tensor(out=sbb[:], in0=sbb[:], in1=ps[bs + 1][:, :], op=ALU.mult), r=[r_ps[bs + 1]], w=[r_sb])
                k.op("dve", lambda e: e.tensor_tensor(out=mergedT[:, j, a0:a0 + 512], in0=sa[:], in1=sbb[:], op=ALU.add), r=[r_sa, r_sb], w=[r_mer])
    k.barrier()
    off = offD1
    x2, off = T("x2", [128, 8, D], F32, off)
    r_x2 = [k.res() for _ in range(8)]
    offX = off
    slabs = [None, None]
    slabs[0], off = T("slabD0", [128, 16, 512], BF16, off)
    slabs[1], off = T("slabD1", [128, 16, 512], BF16, off)
    r_slabs = [k.res(), k.res()]
    offS = off
    assert off <= SB_LIMIT, off
    for t in range(8):
        k.dma("sp", x2[:, t, :], xo[t * 128:(t + 1) * 128, :], w=[r_x2[t]])
    for sl in range(4):
        slab, r_sl = load_slab(slabs, r_slabs, w_out_v[:, :, sl * 512:(sl + 1) * 512])
        for t in range(8):
            bk = next_bank(8)
            mm_tm(bk, mergedT, t * 128, slab, 512, r=[r_sl, r_mer])
            k.op("dve", lambda e: e.tensor_tensor(out=x2[:, t, sl * 512:(sl + 1) * 512], in0=x2[:, t, sl * 512:(sl + 1) * 512], in1=ps[bk][:, :], op=ALU.add),
                 r=[r_ps[bk]], w=[r_x2[t]])
    if dbg:
        k.dma("sp", x2_d[:, :], x2[:].rearrange("p t d -> p (t d)"), r=r_x2)
    if stop_after == "D":
        k.barrier()
        return nc

    pre_slab = load_slab(slabs, r_slabs, w_up_v[:, :, 0:512])
    k.barrier()
    k.dma("sp", gbc[:], g_mlp[:, :], w=[r_gbc])
    off = offC0
    h2T, off = T("h2T", [128, 16, TOK], BF16, off)
    r_h2 = [k.res() for _ in range(8)]
    assert off <= offD1
    off = offS
    upT, off = T("upT", [128, 16, TOK], BF16, off)
    r_up = k.res()
    assert off <= SB_LIMIT, off
    off = P0
    xn[0], off = T("xnE0", [128, D], BF16, off)
    xn[1], off = T("xnE1", [128, D], BF16, off)
    junk, off = T("junkE", [128, D], BF16, off)
    rl = [None, None]
    rl[0], off = T("rl0", [128, 512], F32, off)
    rl[1], off = T("rl1", [128, 512], F32, off)
    r_rl = [k.res(), k.res()]
    assert off <= offC0, off
    tmpE = ([None, None], [None, None], xn, [k.res(), k.res()], junk, k.res())

    def src_x2(t, xb, rxb):
        return x2[:, t, :], r_x2[t]

    prep_hT(src_x2, 8, h2T, r_h2, tmpE)
    ri = 0
    for fg in range(4):
        for sl in range(4):
            if fg == 0 and sl == 0:
                slab, r_sl = pre_slab
            else:
                slab, r_sl = load_slab(slabs, r_slabs, w_up_v[:, :, (fg * 4 + sl) * 512:(fg * 4 + sl + 1) * 512])
            for fc in range(4):
                for half in range(2):
                    bk = next_bank()
                    mm_fm(bk, 512, slab, fc * 128, h2T, half * 512, r=[r_sl] + r_h2[half * 4:half * 4 + 4])
                    i = ri % 2
                    ri += 1
                    k.op("act", lambda e: e.activation(out=rl[i][:], in_=ps[bk][:, :], func=AF.Relu), r=[r_ps[bk]], w=[r_rl[i]])
                    k.op("dve", lambda e: e.tensor_tensor(out=upT[:, sl * 4 + fc, half * 512:(half + 1) * 512], in0=rl[i][:], in1=rl[i][:], op=ALU.mult), r=[r_rl[i]], w=[r_up])
        wd_v = w_down[fg * 2048:(fg + 1) * 2048, :].rearrange("(c p) n -> p c n", p=128)
        for sl in range(4):
            slab, r_sl = load_slab(slabs, r_slabs, wd_v[:, :, sl * 512:(sl + 1) * 512])
            for t in range(8):
                bk = next_bank()
                mm_tm(bk, upT, t * 128, slab, 512, r=[r_sl, r_up])
                k.op("dve", lambda e: e.tensor_tensor(out=x2[:, t, sl * 512:(sl + 1) * 512], in0=x2[:, t, sl * 512:(sl + 1) * 512], in1=ps[bk][:, :], op=ALU.add),
                     r=[r_ps[bk]], w=[r_x2[t]])

    gfin, _ = T("gfin", [128, D], F32, P0)
    r_gfin = k.res()
    k.dma("sp", gfin[:], g_fin[:, :], w=[r_gfin, tmpE[3][0], tmpE[3][1]])
    off = offC0
    ot = [None, None]
    ot[0], off = T("ot0", [128, D], F32, off)
    ot[1], off = T("ot1", [128, D], F32, off)
    r_ot = [k.res(), k.res()]
    for t in range(8):
        st, r_st = next_stat()
        k.op("dve", lambda e: e.memset(st[:, 0:2], 0.0), w=[r_st])
        k.op("act", lambda e: e.activation(out=junk[:], in_=x2[:, t, :], func=AF.Square, accum_out=st[:, 0:1]), r=[r_x2[t]], w=[tmpE[5], r_st])
        rstd_from_ss(st[:, 0:1], st[:, 1:2], r_st, D)
        b = t % 2
        k.op("dve", lambda e: e.scalar_tensor_tensor(out=ot[b][:], in0=x2[:, t, :], scalar=st[:, 1:2], in1=gfin[:], op0=ALU.mult, op1=ALU.mult),
             r=[r_x2[t], r_st, r_gfin], w=[r_ot[b]] + (r_h2 if t < 2 else []))
        k.dma("sp", out[t * 128:(t + 1) * 128, :], ot[b][:], r=[r_ot[b]])
    k.barrier()
    return nc


def _t5_bucket(n):
    n = np.maximum(n, 0)
    nf = np.maximum(n, 1).astype(np.float32)
    large = 16 + (np.log(nf / np.float32(16)) / np.float32(math.log(128 / 16)) * np.float32(16)).astype(np.int32)
    large = np.minimum(large, 31)
    return np.where(n < 16, n, large)


def core_blocks(c):
    return [c, 8 + c, 16 + c, 24 + c]


def make_inputs(c, x, ln_mix, w_in, a_v_gain, a_spatial, a_spatial_bias, w_proj_a, w_proj_b,
                w_out, rel_bias, ln_mlp, w_up, w_down, ln_final, shared=None):
    x2d = np.asarray(x, np.float32).reshape(S, D)
    blocks = core_blocks(c)
    if shared is None:
        shared = {}
        rep = lambda v: np.ascontiguousarray(np.broadcast_to(np.asarray(v, np.float32).reshape(1, -1), (128, np.asarray(v).size)))
        shared["w_in"] = np.ascontiguousarray(np.asarray(w_in, np.float32)[0])
        shared["w_pa"] = np.ascontiguousarray(np.asarray(w_proj_a, np.float32)[0])
        shared["w_pb"] = np.ascontiguousarray(np.asarray(w_proj_b, np.float32)[0])
        shared["w_out"] = np.ascontiguousarray(np.asarray(w_out, np.float32)[0])
        shared["w_up"] = np.ascontiguousarray(np.asarray(w_up, np.float32)[0])
        shared["w_down"] = np.ascontiguousarray(np.asarray(w_down, np.float32)[0])
        shared["g_mix"] = rep(np.asarray(ln_mix)[0])
        shared["g_v"] = rep(np.asarray(a_v_gain)[0])
        shared["g_mlp"] = rep(np.asarray(ln_mlp)[0])
        shared["g_fin"] = rep(np.asarray(ln_final))
        asp = np.asarray(a_spatial, np.float32)[0]
        shared["wsT"] = np.ascontiguousarray(asp.transpose(2, 0, 1).reshape(128, 16 * 128))
        s_i = np.arange(128)[:, None]
        t_i = np.arange(128)[None, :]
        shared["cmask"] = (s_i <= t_i).astype(np.float32)
        bs = np.asarray(a_spatial_bias, np.float32)[0]
        shared["bsb"] = np.ascontiguousarray(np.broadcast_to(bs.reshape(1, 16 * 128), (128, 16 * 128)))
        rb = np.asarray(rel_bias, np.float32)
        kk = np.arange(128)[:, None]
        qq = np.arange(128)[None, :]
        bd = _t5_bucket(qq - kk)
        td = rb[bd, :]
        td = np.where((qq >= kk)[:, :, None], td, rb[31][None, None, :])
        shared["tdg"] = np.ascontiguousarray(td.transpose(0, 2, 1).reshape(128, 16 * 128))
        bp = _t5_bucket(qq - kk + 128)
        tp = rb[bp, :]
        shared["tpv"] = np.ascontiguousarray(tp.transpose(0, 2, 1).reshape(128, 16 * 128))
        shared["cfar"] = np.ascontiguousarray(np.broadcast_to(rb[31].reshape(1, 16), (128, 16)))
        shared["negm"] = np.where(qq >= kk, 0.0, NEGV).astype(np.float32)
        shared["idn"] = np.eye(128, dtype=np.float32).astype(ml_dtypes.bfloat16)
    m = dict(shared)
    vorder = [None] * 32
    fixed = set()
    for s in range(4):
        vorder[8 * s + 7] = blocks[s]
        fixed.add(blocks[s])
        if blocks[s] > 0:
            vorder[8 * s + 6] = blocks[s] - 1
            fixed.add(blocks[s] - 1)
    rest = [r for r in range(32) if r not in fixed]
    for p in range(32):
        if vorder[p] is None:
            vorder[p] = rest.pop(0)
    assert sorted(vorder) == list(range(32))
    for s in range(4):
        assert all(vorder.index(r) < 8 * s + 7 for r in range(blocks[s]))
    rows = [x2d[r * LB:(r + 1) * LB] for r in vorder]
    m["xa"] = np.ascontiguousarray(np.concatenate(rows, 0))
    m["xo"] = np.ascontiguousarray(np.concatenate([x2d[b * LB:(b + 1) * LB] for b in blocks], 0))
    pm = np.zeros((4, 32), np.float32)
    oh = np.zeros((4, 32), np.float32)
    for s, b in enumerate(blocks):
        for v, r in enumerate(vorder):
            if r >= b:
                pm[s, v] = -1.0e30
            if r == b - 1:
                oh[s, v] = 1.0
    m["pm"] = np.ascontiguousarray(np.broadcast_to(pm.reshape(1, 128), (128, 128)))
    m["oh"] = np.ascontiguousarray(np.broadcast_to(oh.reshape(1, 128), (128, 128)))
    return m, shared


_NC_CACHE = {}


def kernel(**inputs):
    if "nc" not in _NC_CACHE:
        _NC_CACHE["nc"] = build()
    nc = _NC_CACHE["nc"]
    in_maps = []
    shared = None
    for c in range(8):
        m, shared = make_inputs(c, shared=shared, **inputs)
        in_maps.append(m)
    res = run_bass_kernel_spmd(nc, in_maps, core_ids=list(range(8)))
    outp = np.zeros((S, D), np.float32)
    for c in range(8):
        o = np.asarray(res.results[c]["out"], np.float32)
        for s, b in enumerate(core_blocks(c)):
            outp[b * LB:(b + 1) * LB] = o[s * LB:(s + 1) * LB]
    return outp.reshape(1, S, D)
```

```python
import os
import math
import numpy as np
import ml_dtypes
import concourse.bass as bass
import concourse.mybir as mybir
from concourse.bass_utils import run_bass_kernel_spmd

F32 = mybir.dt.float32
BF16 = mybir.dt.bfloat16
AF = mybir.ActivationFunctionType
ALU = mybir.AluOpType
AX = mybir.AxisListType

D = 2048
S = 8192
NH = 16
HD = 128
LB = 256
NBLK = 32
FF = 8192
INC = 14336
TOK = 1024
NPAST = 31 * LB
NVT = S
EPS = 1e-6
SCALE = HD ** -0.5
NEGV = -1.0e5
GROUPS = [(i * 8, 8) for i in range(8)]
RS = [7, 15, 23, 31]
ACT_SHARE = int(os.environ.get('KACT_SHARE', '0'))
PD = int(os.environ.get('KPD', '3'))


class Ev:
    __slots__ = ("key", "sem", "val")

    def __init__(self, key, sem, val):
        self.key = key
        self.sem = sem
        self.val = val


class Res:
    __slots__ = ("name", "w", "r")

    def __init__(self, name):
        self.name = name
        self.w = None
        self.r = {}


class Eng:
    def __init__(self, name, obj):
        self.name = name
        self.obj = obj
        self.sem = None
        self.cnt = 0
        self.epoch = 0
        self.seen = {}
        self.last = None


class K:
    EPOCH = 30000
    NDMA = 16

    def __init__(self, nc):
        self.nc = nc
        self.eng = {
            "pe": Eng("pe", nc.tensor),
            "act": Eng("act", nc.scalar),
            "dve": Eng("dve", nc.vector),
            "pool": Eng("pool", nc.gpsimd),
            "sp": Eng("sp", nc.sync),
        }
        for e in self.eng.values():
            e.sem = nc.alloc_semaphore(f"s_{e.name}_0")
        self.dq = {}
        for qn in ("sp", "pool"):
            self.dq[qn] = {"sem": [nc.alloc_semaphore(f"s_dma_{qn}_{i}") for i in range(self.NDMA)], "cnt": [0] * self.NDMA, "next": 0}
        self.nres = 0

    def res(self, name=None):
        self.nres += 1
        return Res(name or f"r{self.nres}")

    def _wait(self, e, ev):
        if ev is None:
            return
        if e.name == "pe" and ev.key[0] == "pe":
            return
        if e.seen.get(ev.key, 0) >= ev.val:
            return
        e.obj.wait_ge(ev.sem, ev.val)
        e.seen[ev.key] = ev.val

    def _deps(self, e, r, w):
        for x in r:
            self._wait(e, x.w)
        for x in w:
            self._wait(e, x.w)
            for ev in x.r.values():
                self._wait(e, ev)

    def _mark(self, ev, r, w):
        for x in r:
            x.r[ev.key] = ev
        for x in w:
            x.w = ev
            x.r = {}

    def _signal(self, e, ins):
        if e.cnt >= self.EPOCH:
            e.epoch += 1
            e.cnt = 0
            e.sem = self.nc.alloc_semaphore(f"s_{e.name}_{e.epoch}")
        e.cnt += 1
        ins.then_inc(e.sem, 1)
        ev = Ev((e.name, e.epoch), e.sem, e.cnt)
        e.last = ev
        return ev

    def op(self, en, fn, r=(), w=()):
        e = self.eng[en]
        self._deps(e, r, w)
        ins = fn(e.obj)
        ev = self._signal(e, ins)
        self._mark(ev, r, w)
        return ev

    def pe(self, fns, r=(), w=()):
        e = self.eng["pe"]
        self._deps(e, r, w)
        ins = None
        for fn in fns:
            ins = fn(e.obj)
        ev = self._signal(e, ins)
        self._mark(ev, r, w)
        return ev

    def dma(self, qn, out, in_, r=(), w=()):
        e = self.eng[qn]
        q = self.dq[qn]
        self._deps(e, r, w)
        i = q["next"]
        q["next"] = (i + 1) % self.NDMA
        if q["cnt"][i] > 0:
            self._wait(e, Ev(("dma", qn, i), q["sem"][i], q["cnt"][i] * 16))
        q["cnt"][i] += 1
        e.obj.dma_start(out=out, in_=in_).then_inc(q["sem"][i], 16)
        ev = Ev(("dma", qn, i), q["sem"][i], q["cnt"][i] * 16)
        self._mark(ev, r, w)
        return ev

    def _wait_any(self, e, ev):
        if e.seen.get(ev.key, 0) >= ev.val:
            return
        e.obj.wait_ge(ev.sem, ev.val)
        e.seen[ev.key] = ev.val

    def barrier(self):
        evs = [e.last for e in self.eng.values() if e.last is not None]
        for qn, q in self.dq.items():
            for i in range(self.NDMA):
                if q["cnt"][i] > 0:
                    evs.append(Ev(("dma", qn, i), q["sem"][i], q["cnt"][i] * 16))
        for e in self.eng.values():
            for ev in evs:
                self._wait_any(e, ev)


def build(stop_after=None, dbg=False):
    nc = bass.Bass("TRN2", target_bir_lowering=False)
    k = K(nc)

    def din(name, shape, dt=F32):
        return nc.dram_tensor(name, shape, dt, kind="ExternalInput").ap()

    xa = din("xa", [NVT, D])
    xo = din("xo", [TOK, D])
    w_in = din("w_in", [D, INC])
    w_pa = din("w_pa", [D, D])
    w_pb = din("w_pb", [D, D])
    w_out = din("w_out", [D, D])
    w_up = din("w_up", [D, FF])
    w_down = din("w_down", [FF, D])
    g_mix = din("g_mix", [128, D])
    g_v = din("g_v", [128, D])
    g_mlp = din("g_mlp", [128, D])
    g_fin = din("g_fin", [128, D])
    wsT = din("wsT", [128, 16 * 128])
    cmask = din("cmask", [128, 128])
    bsb = din("bsb", [128, 16 * 128])
    tdg = din("tdg", [128, 16 * 128])
    tpv = din("tpv", [128, 16 * 128])
    cfar = din("cfar", [128, 16])
    negm = din("negm", [128, 128])
    pmd = din("pm", [128, 128])
    ohd = din("oh", [128, 128])
    idn = din("idn", [128, 128], BF16)
    out = nc.dram_tensor("out", [TOK, D], F32, kind="ExternalOutput").ap()

    skind = "ExternalOutput" if dbg else "Internal"
    kT_d = nc.dram_tensor("kT_d", [NH, 128, NVT], BF16, kind=skind).ap()
    v_d = nc.dram_tensor("v_d", [NVT, D], BF16, kind=skind).ap()
    qT_d = nc.dram_tensor("qT_d", [NH, 128, TOK], BF16, kind=skind).ap()
    hTo_d = nc.dram_tensor("hTo_d", [128, 16 * TOK], BF16, kind=skind).ap()
    mix_d = nc.dram_tensor("mix_d", [128, 16 * TOK], BF16, kind=skind).ap()
    ksum_d = nc.dram_tensor("ksum_d", [128, 16 * 40], F32, kind=skind).ap()
    att_d = nc.dram_tensor("att_d", [128, 16 * TOK], BF16, kind=skind).ap() if dbg else None
    x2_d = nc.dram_tensor("x2_d", [128, 8 * D], F32, kind=skind).ap() if dbg else None

    w_in_v = w_in.rearrange("(c p) n -> p c n", p=128)
    w_pa_v = w_pa.rearrange("(c p) n -> p c n", p=128)
    w_pb_v = w_pb.rearrange("(c p) n -> p c n", p=128)
    w_out_v = w_out.rearrange("(c p) n -> p c n", p=128)
    w_up_v = w_up.rearrange("(c p) n -> p c n", p=128)

    uid = [0]

    def T(name, shape, dt, off):
        uid[0] += 1
        n = 1
        for s_ in shape[1:]:
            n *= s_
        nbytes = n * (4 if dt == F32 else 2)
        t = nc.alloc_sbuf_tensor_at(f"{name}_{uid[0]}", shape, dt, offset=off)
        return t, off + ((nbytes + 63) // 64) * 64

    ps = [nc.alloc_psum_tensor(f"ps{i}", [128, 512], F32) for i in range(8)]
    r_ps = [k.res(f"ps{i}") for i in range(8)]

    off = 16512
    ident, off = T("ident", [128, 128], BF16, off)
    stat, off = T("stat", [128, 64], F32, off)
    gbc, off = T("gbc", [128, D], F32, off)
    r_ident, r_gbc = k.res(), k.res()
    ksum, off = T("ksum", [128, 16, 40], F32, off)
    r_ksum = k.res()
    P0 = off
    SB_LIMIT = 229280

    k.dma("sp", ident[:], idn[:, :], w=[r_ident])
    k.dma("sp", gbc[:], g_mix[:, :], w=[r_gbc])
    k.op("dve", lambda e: e.memset(ksum[:], 0.0), w=[r_ksum])

    stat_slots = [k.res() for _ in range(8)]
    stat_i = [0]

    def next_stat():
        i = stat_i[0] % 8
        stat_i[0] += 1
        return stat[:, 8 * i:8 * i + 8], stat_slots[i]

    bank_i = [0]

    def next_bank(n=6):
        b = bank_i[0] % n
        bank_i[0] += 1
        return b

    evac_i = [0]

    def evac_copy(dst, src, r, w):
        evac_i[0] += 1
        if evac_i[0] % 2:
            return k.op("act", lambda e: e.activation(out=dst, in_=src, func=AF.Copy), r=r, w=w)
        return k.op("dve", lambda e: e.tensor_copy(out=dst, in_=src), r=r, w=w)

    def rstd_from_ss(ss_ap, out_ap, r_st, n):
        k.op("dve", lambda e: e.tensor_scalar(out=out_ap, in0=ss_ap, scalar1=1.0 / n, scalar2=EPS, op0=ALU.mult, op1=ALU.add), r=[r_st], w=[r_st])
        k.op("act", lambda e: e.activation(out=out_ap, in_=out_ap, func=AF.Sqrt), r=[r_st], w=[r_st])
        k.op("dve", lambda e: e.reciprocal(out=out_ap, in_=out_ap), r=[r_st], w=[r_st])

    def prep_load(src_fn, t, tmp):
        xbuf, r_xb = tmp[0], tmp[1]
        return src_fn(t, xbuf[t % 2], r_xb[t % 2])

    def prep_norm(loaded, t, tmp):
        xbuf, r_xb, xn, r_xn, junk, r_junk = tmp
        b = t % 2
        xin, r_in = loaded
        st, r_st = next_stat()
        k.op("dve", lambda e: e.memset(st[:, 0:2], 0.0), w=[r_st])
        k.op("act", lambda e: e.activation(out=junk[:], in_=xin, func=AF.Square, accum_out=st[:, 0:1]), r=[r_in], w=[r_junk, r_st])
        rstd_from_ss(st[:, 0:1], st[:, 1:2], r_st, D)
        k.op("dve", lambda e: e.scalar_tensor_tensor(out=xn[b][:], in0=xin, scalar=st[:, 1:2], in1=gbc[:], op0=ALU.mult, op1=ALU.mult),
             r=[r_in, r_st, r_gbc], w=[r_xn[b]])

    def prep_tr(t, hT, r_hT_tiles, tmp):
        xbuf, r_xb, xn, r_xn, junk, r_junk = tmp
        b = t % 2
        for half in range(2):
            pb = ps[6 + half][:].bitcast(BF16)
            k.pe([(lambda e, c=c: e.transpose(out=pb[:, (c % 8) * 128:(c % 8 + 1) * 128], in_=xn[b][:, c * 128:(c + 1) * 128], identity=ident[:]))
                  for c in range(half * 8, half * 8 + 8)], r=[r_xn[b], r_ident], w=[r_ps[6 + half]])
            src = pb.rearrange("p (c t) -> p c t", c=8)
            dst = hT[:, half * 8:half * 8 + 8, t * 128:(t + 1) * 128]
            if half == 0:
                k.op("act", lambda e: e.activation(out=dst, in_=src, func=AF.Copy), r=[r_ps[6]], w=[r_hT_tiles[t]])
            else:
                k.op("dve", lambda e: e.tensor_copy(out=dst, in_=src), r=[r_ps[7]], w=[r_hT_tiles[t]])

    def prep_tile(src_fn, t, hT, r_hT_tiles, tmp):
        ld = prep_load(src_fn, t, tmp)
        prep_norm(ld, t, tmp)
        prep_tr(t, hT, r_hT_tiles, tmp)

    def prep_hT(src_fn, ntiles, hT, r_hT_tiles, tmp):
        for t in range(ntiles):
            prep_tile(src_fn, t, hT, r_hT_tiles, tmp)

    def mm_fm(bank, N, W, wc0, act, a0, r, nchunk=16):
        k.pe([(lambda e, c=c: e.matmul(ps[bank][:, 0:N], lhsT=W[:, c, wc0:wc0 + 128], rhs=act[:, c, a0:a0 + N], start=(c == 0), stop=(c == nchunk - 1)))
              for c in range(nchunk)], r=r, w=[r_ps[bank]])

    def mm_tm(bank, act, a0, W, N, r, nchunk=16):
        k.pe([(lambda e, c=c: e.matmul(ps[bank][:, 0:N], lhsT=act[:, c, a0:a0 + 128], rhs=W[:, c, 0:N], start=(c == 0), stop=(c == nchunk - 1)))
              for c in range(nchunk)], r=r, w=[r_ps[bank]])

    slab_i = [0]

    def load_slab(slabs, r_slabs, view):
        b = slab_i[0] % 2
        slab_i[0] += 1
        k.dma("pool", slabs[b][:], view, w=[r_slabs[b]])
        return slabs[b], r_slabs[b]

    off = P0
    hTb = [None, None]
    hTb[0], off = T("hT0", [128, 16, TOK], BF16, off)
    hTb[1], off = T("hT1", [128, 16, TOK], BF16, off)
    r_hT = [[k.res() for _ in range(8)] for _ in range(2)]
    xbuf = [None, None]
    xbuf[0], off = T("xb0", [128, D], F32, off)
    xbuf[1], off = T("xb1", [128, D], F32, off)
    r_xb = [k.res(), k.res()]
    xn = [None, None]
    xn[0], off = T("xn0", [128, D], BF16, off)
    xn[1], off = T("xn1", [128, D], BF16, off)
    r_xn = [k.res(), k.res()]
    junk, off = T("junk", [128, D], BF16, off)
    r_junk = k.res()
    tmpA = (xbuf, r_xb, xn, r_xn, junk, r_junk)
    slabs = [None, None]
    slabs[0], off = T("slab0", [128, 16, 512], BF16, off)
    slabs[1], off = T("slab1", [128, 16, 512], BF16, off)
    r_slabs = [k.res(), k.res()]
    offA = off
    kst = [None, None]
    kst[0], off = T("kst0", [128, 4, TOK], BF16, off)
    kst[1], off = T("kst1", [128, 4, TOK], BF16, off)
    r_kst = [k.res(), k.res()]
    vst = [None, None]
    vst[0], off = T("vst0", [128, 8, 512], BF16, off)
    vst[1], off = T("vst1", [128, 8, 512], BF16, off)
    r_vst = [k.res(), k.res()]
    assert off <= SB_LIMIT, off

    kT_dv = kT_d.rearrange("h d t -> d h t")
    st_i = 0
    def src_a_of(t0):
        def src_a(t, xb, rxb):
            k.dma("sp", xb[:], xa[(t0 + t) * 128:(t0 + t + 1) * 128, :], w=[rxb])
            return xb[:], rxb
        return src_a

    def src_o(t, xb, rxb):
        k.dma("sp", xb[:], xo[t * 128:(t + 1) * 128, :], w=[rxb])
        return xb[:], rxb

    NG = len(GROUPS)
    prep_hT(src_a_of(0), GROUPS[0][1], hTb[NG % 2], r_hT[NG % 2], tmpA)
    for gi, (t0, nt) in enumerate(GROUPS):
        hT = hTb[(gi + NG) % 2]
        rh = r_hT[(gi + NG) % 2]
        NT = nt * 128
        if gi + 1 < len(GROUPS):
            nxt = (src_a_of(GROUPS[gi + 1][0]), GROUPS[gi + 1][1], hTb[(gi + 1 + NG) % 2], r_hT[(gi + 1 + NG) % 2])
        else:
            nxt = (src_o, 8, hTb[(gi + 1 + NG) % 2], r_hT[(gi + 1 + NG) % 2])
        if gi == 0:
            pending = prep_load(nxt[0], 0, tmpA)
        for sl in range(8):
            do_prep = sl < nxt[1]
            if do_prep:
                prep_norm(pending, sl, tmpA)
            if sl + 1 < nxt[1]:
                pending = prep_load(nxt[0], sl + 1, tmpA)
            slab, r_sl = load_slab(slabs, r_slabs, w_in_v[:, :, 6144 + sl * 512:6144 + (sl + 1) * 512])
            sb = st_i % 2
            st_i += 1
            if sl < 4:
                for hh in range(4):
                    for a0 in range(0, NT, 512):
                        N = min(512, NT - a0)
                        bk = next_bank()
                        mm_fm(bk, N, slab, hh * 128, hT, a0, r=[r_sl] + rh[a0 // 128:(a0 + N) // 128])
                        for bo in range(0, N, LB):
                            vb = (t0 * 128 + a0 + bo) // LB
                            k.op("act", lambda e, bo=bo, vb=vb: e.activation(out=kst[sb][:, hh, a0 + bo:a0 + bo + LB], in_=ps[bk][:, bo:bo + LB], func=AF.Copy,
                                                                             accum_out=ksum[:, sl * 4 + hh, vb:vb + 1]),
                                 r=[r_ps[bk]], w=[r_kst[sb], r_ksum])
                k.dma("sp", kT_dv[:, sl * 4:(sl + 1) * 4, t0 * 128:t0 * 128 + NT], kst[sb][:, :, 0:NT], r=[r_kst[sb]])
            else:
                for t in range(nt):
                    bk = next_bank()
                    mm_tm(bk, hT, t * 128, slab, 512, r=[r_sl, rh[t]])
                    k.op("dve", lambda e: e.tensor_copy(out=vst[sb][:, t, :], in_=ps[bk][:, :]), r=[r_ps[bk]], w=[r_vst[sb]])
                k.dma("sp", v_d[t0 * 128:t0 * 128 + NT, (sl - 4) * 512:(sl - 3) * 512].rearrange("(n p) c -> p n c", p=128),
                      vst[sb][:, 0:nt, :], r=[r_vst[sb]])
            if do_prep:
                prep_tr(sl, nxt[2], nxt[3], tmpA)
            if sl == 7 and gi + 2 < len(GROUPS) + 1:
                if gi + 2 < len(GROUPS):
                    pending = prep_load(src_a_of(GROUPS[gi + 2][0]), 0, tmpA)
                else:
                    pending = prep_load(src_o, 0, tmpA)
    OWN_BUF = (2 * NG) % 2
    k.dma("sp", ksum_d[:, :], ksum[:].rearrange("p h n -> p (h n)"), r=[r_ksum])

    if stop_after == "A":
        k.barrier()
        return nc

    k.barrier()
    assert OWN_BUF == 0
    hTo = hTb[0]
    rho = r_hT[0]
    off = P0 + 32 * 1024
    vg, off = T("vg", [128, 8, D], F32, off)
    r_vg = [k.res() for _ in range(8)]
    offB1 = off
    vln, off = T("vln", [128, 8, D], BF16, off)
    r_vln = [k.res() for _ in range(8)]
    slabs = [None, None]
    slabs[0], off = T("slabB0", [128, 16, 512], BF16, off)
    slabs[1], off = T("slabB1", [128, 16, 512], BF16, off)
    r_slabs = [k.res(), k.res()]
    vgbc, off = T("vgbc", [128, D], F32, off)
    r_vgbc = k.res()
    wTm, off = T("wTm", [128, 16, 128], BF16, off)
    r_wTm = k.res()
    bbc, off = T("bbc", [128, 16 * 128], F32, off)
    r_bbc = k.res()
    st6, off = T("st6", [128, 4, 6], F32, off)
    r_st6 = k.res()
    qst = [None, None]
    qst[0], off = T("qst0", [128, 4, TOK], BF16, off)
    qst[1], off = T("qst1", [128, 4, TOK], BF16, off)
    r_qst = [k.res(), k.res()]
    assert off <= SB_LIMIT, off
    k.dma("sp", vgbc[:], g_v[:, :], w=[r_vgbc])
    k.dma("sp", bbc[:], bsb[:, :], w=[r_bbc])
    wtmp = vg[:, 0, :]
    k.dma("sp", wtmp, wsT[:, :], w=[r_vg[0]])
    cm = vg[:, 1, 0:128]
    k.dma("sp", cm, cmask[:, :], w=[r_vg[1]])
    k.op("dve", lambda e: e.tensor_tensor(out=wTm[:], in0=wtmp.rearrange("p (g t) -> p g t", g=16),
                                          in1=cm.unsqueeze(1).broadcast_to([128, 16, 128]), op=ALU.mult),
         r=[r_vg[0], r_vg[1]], w=[r_wTm])

    for sl in range(4):
        slab, r_sl = load_slab(slabs, r_slabs, w_in_v[:, :, 2048 + sl * 512:2048 + (sl + 1) * 512])
        for t in range(8):
            bk = next_bank()
            mm_tm(bk, hTo, t * 128, slab, 512, r=[r_sl, rho[t]])
            k.op("act", lambda e: e.activation(out=vg[:, t, sl * 512:(sl + 1) * 512], in_=ps[bk][:, :], func=AF.Gelu), r=[r_ps[bk]], w=[r_vg[t]])
    qT_dv = qT_d.rearrange("h d t -> d h t")

    def q_chunk(i):
        sl, hp = i // 2, i % 2
        sb = sl % 2
        if hp == 0:
            q_chunk.cur = load_slab(slabs, r_slabs, w_in_v[:, :, 4096 + sl * 512:4096 + (sl + 1) * 512])
        slab, r_sl = q_chunk.cur
        for hh in (2 * hp, 2 * hp + 1):
            for half in range(2):
                bk = next_bank()
                mm_fm(bk, 512, slab, hh * 128, hTo, half * 512, r=[r_sl] + rho[half * 4:half * 4 + 4])
                k.op("act", lambda e: e.activation(out=qst[sb][:, hh, half * 512:(half + 1) * 512], in_=ps[bk][:, :], func=AF.Copy), r=[r_ps[bk]], w=[r_qst[sb]])
        if hp == 1:
            k.dma("sp", qT_dv[:, sl * 4:(sl + 1) * 4, :], qst[sb][:], r=[r_qst[sb]])

    for t in range(8):
        st, r_st = next_stat()
        for c in range(4):
            k.op("dve", lambda e, c=c: e.bn_stats(out=st6[:, c, :], in_=vg[:, t, c * 512:(c + 1) * 512]), r=[r_vg[t]], w=[r_st6])
        k.op("dve", lambda e: e.bn_aggr(out=st[:, 0:2], in_=st6[:]), r=[r_st6], w=[r_st])
        rstd_from_ss(st[:, 1:2], st[:, 2:3], r_st, 1.0)
        k.op("dve", lambda e: e.tensor_scalar(out=vg[:, t, :], in0=vg[:, t, :], scalar1=st[:, 0:1], scalar2=st[:, 2:3], op0=ALU.subtract, op1=ALU.mult),
             r=[r_st], w=[r_vg[t]])
        k.op("dve", lambda e: e.tensor_tensor(out=vln[:, t, :], in0=vg[:, t, :], in1=vgbc[:], op=ALU.mult), r=[r_vg[t], r_vgbc], w=[r_vln[t]])
        q_chunk(t)
    k.barrier()
    off = P0 + 32 * 1024
    mixT, off = T("mixT", [128, 16, TOK], BF16, off)
    r_mix = k.res()
    gu = [None, None]
    gu[0], off = T("gu0", [128, 512], F32, off)
    gu[1], off = T("gu1", [128, 512], F32, off)
    r_gu = [k.res(), k.res()]
    t2 = [None, None]
    t2[0], off = T("t20", [128, 512], F32, off)
    t2[1], off = T("t21", [128, 512], F32, off)
    r_t2 = [k.res(), k.res()]
    assert off <= offB1, (off, offB1)
    ui = 0
    for g in range(16):
        if g % 4 == 0:
            slab, r_sl = load_slab(slabs, r_slabs, w_in_v[:, :, (g // 4) * 512:(g // 4 + 1) * 512])
        for half in range(2):
            bu = next_bank()
            mm_fm(bu, 512, slab, (g % 4) * 128, hTo, half * 512, r=[r_sl] + rho[half * 4:half * 4 + 4])
            bs = next_bank()
            k.pe([(lambda e, t=t: e.matmul(ps[bs][:, (t % 4) * 128:(t % 4 + 1) * 128], lhsT=vln[:, t, g * 128:(g + 1) * 128], rhs=wTm[:, g, :], start=True, stop=True))
                  for t in range(half * 4, half * 4 + 4)], r=[r_vln[t] for t in range(half * 4, half * 4 + 4)] + [r_wTm], w=[r_ps[bs]])
            i = ui % 2
            ui += 1
            k.op("act", lambda e: e.activation(out=gu[i][:], in_=ps[bu][:, :], func=AF.Gelu), r=[r_ps[bu]], w=[r_gu[i]])
            k.op("dve", lambda e: e.tensor_tensor(out=t2[i][:].rearrange("p (a b) -> p a b", a=4), in0=ps[bs][:, :].rearrange("p (a b) -> p a b", a=4),
                                                  in1=bbc[:, g * 128:(g + 1) * 128].unsqueeze(1).broadcast_to([128, 4, 128]), op=ALU.add),
                 r=[r_ps[bs], r_bbc], w=[r_t2[i]])
            k.op("dve", lambda e: e.tensor_tensor(out=mixT[:, g, half * 512:(half + 1) * 512], in0=t2[i][:], in1=gu[i][:], op=ALU.mult),
                 r=[r_t2[i], r_gu[i]], w=[r_mix])
    k.dma("sp", hTo_d[:, :], hTo[:].rearrange("p c t -> p (c t)"), r=rho)
    k.dma("sp", mix_d[:, :], mixT[:].rearrange("p c t -> p (c t)"), r=[r_mix])
    if stop_after == "B":
        k.barrier()
        return nc

    k.barrier()
    off = P0
    attnT, off = T("attnT", [128, 16, TOK], BF16, off)
    r_att = k.res()
    offC0 = off
    BdT, off = T("BdT", [128, 16, 128], BF16, off)
    BpT, off = T("BpT", [128, 16, 128], BF16, off)
    Ecr, off = T("Ecr", [128, 16, 128], BF16, off)
    r_bias = k.res()
    pm_sb, off = T("pm_sb", [128, 128], F32, off)
    oh_sb, off = T("oh_sb", [128, 128], F32, off)
    r_pm = k.res()
    kTb, vab, qTb = [None, None], [None, None], [None, None]
    r_kv = [k.res(), k.res()]
    for b in range(2):
        kTb[b], off = T(f"kT{b}", [128, NVT], BF16, off)
        vab[b], off = T(f"va{b}", [128, NVT // 128, 130], BF16, off)
        qTb[b], off = T(f"qT{b}", [128, TOK], BF16, off)
    km, kmb, kml, gsm, sel, m8, selp, stmp = [[None, None] for _ in range(8)]
    r_km = [k.res(), k.res()]
    r_sel = [k.res(), k.res()]
    for b in range(2):
        km[b], off = T(f"km{b}", [128, 32], F32, off)
        kmb[b], off = T(f"kmb{b}", [128, 32], BF16, off)
        kml[b], off = T(f"kml{b}", [128, 32], BF16, off)
        gsm[b], off = T(f"gsm{b}", [128, 8, 32], F32, off)
        sel[b], off = T(f"sel{b}", [128, 8, 32], F32, off)
        m8[b], off = T(f"m8{b}", [128, 8, 8], F32, off)
        selp[b], off = T(f"selp{b}", [128, 8], F32, off)
        stmp[b], off = T(f"stmp{b}", [128, 32], F32, off)
    tma = [None] * 4
    r_tma = [k.res() for _ in range(4)]
    for i_ in range(4):
        tma[i_], off = T(f"tma{i_}", [128, 2, 132], F32, off)
    acc, off = T("acc", [128, 8, 132], F32, off)
    r_acc = [k.res() for _ in range(8)]
    PT = [None] * 4
    for i_ in range(4):
        PT[i_], off = T(f"PT{i_}", [128, 512], BF16, off)
    r_PT = [k.res() for _ in range(4)]
    PTl, off = T("PTl", [128, 512], BF16, off)
    r_PTl = k.res()
    obf, off = T("obf", [128, 8, 128], BF16, off)
    r_obf = k.res()
    rec, off = T("rec", [128, 8], F32, off)
    r_rec = k.res()
    tdg_sb, off2 = T("tdg_sb", [128, 16, 128], F32, off)
    tpv_sb, off2 = T("tpv_sb", [128, 16, 128], F32, off2)
    cf_sb, off2 = T("cf_sb", [128, 16], F32, off2)
    ng_sb, off2 = T("ng_sb", [128, 128], F32, off2)
    r_tb = k.res()
    assert off2 <= SB_LIMIT, off2

    k.dma("sp", tdg_sb[:].rearrange("p h q -> p (h q)"), tdg[:, :], w=[r_tb])
    k.dma("sp", tpv_sb[:].rearrange("p h q -> p (h q)"), tpv[:, :], w=[r_tb])
    k.dma("sp", cf_sb[:], cfar[:, :], w=[r_tb])
    k.dma("sp", ng_sb[:], negm[:, :], w=[r_tb])
    k.dma("sp", pm_sb[:], pmd[:, :], w=[r_pm])
    k.dma("sp", oh_sb[:], ohd[:, :], w=[r_pm])
    k.dma("sp", ksum[:].rearrange("p h n -> p (h n)"), ksum_d[:, :], w=[r_ksum])
    for h in range(NH):
        k.op("dve", lambda e, h=h: e.tensor_scalar(out=tdg_sb[:, h, :], in0=tdg_sb[:, h, :], scalar1=cf_sb[:, h:h + 1], scalar2=1.0 / SCALE, op0=ALU.subtract, op1=ALU.mult),
             r=[r_tb], w=[r_tb])
        k.op("dve", lambda e, h=h: e.tensor_scalar(out=tpv_sb[:, h, :], in0=tpv_sb[:, h, :], scalar1=cf_sb[:, h:h + 1], scalar2=1.0 / SCALE, op0=ALU.subtract, op1=ALU.mult),
             r=[r_tb], w=[r_tb])
    k.op("dve", lambda e: e.tensor_tensor(out=BdT[:], in0=tdg_sb[:], in1=ng_sb[:].unsqueeze(1).broadcast_to([128, 16, 128]), op=ALU.add), r=[r_tb], w=[r_bias])
    k.op("dve", lambda e: e.tensor_copy(out=BpT[:], in_=tpv_sb[:]), r=[r_tb], w=[r_bias])
    k.op("act", lambda e: e.activation(out=tpv_sb[:], in_=tpv_sb[:], func=AF.Exp, scale=SCALE), r=[r_tb], w=[r_tb])
    k.op("dve", lambda e: e.tensor_scalar(out=Ecr[:], in0=tpv_sb[:], scalar1=-1.0, scalar2=None, op0=ALU.add), r=[r_tb], w=[r_bias])
    for b in range(2):
        k.op("dve", lambda e, b=b: e.memset(vab[b][:, :, 128:130], 1.0), w=[r_kv[b]])
    for b in range(2):
        k.op("dve", lambda e, b=b: e.memset(km[b][:], 0.0), w=[r_km[b]])

    def head_loads(h):
        b = h % 2
        k.dma("sp", kTb[b][:], kT_d[h, :, :], w=[r_kv[b]])
        k.dma("sp", vab[b][:, :, 0:128], v_d[:, h * 128:(h + 1) * 128].rearrange("(n p) d -> p n d", p=128), w=[r_kv[b]])
        k.dma("sp", qTb[b][:], qT_d[h, :, :], w=[r_kv[b]])

    def gating(h):
        b = h % 2
        kT, qT, rkv = kTb[b], qTb[b], r_kv[b]
        kmh = ksum[:, h, 0:32]
        k.op("dve", lambda e: e.tensor_scalar(out=kmb[b][:], in0=kmh, scalar1=1.0 / LB, scalar2=None, op0=ALU.mult), r=[r_ksum], w=[r_km[b]])
        k.op("dve", lambda e: e.scalar_tensor_tensor(out=kml[b][:], in0=kmh, scalar=1.0 / LB, in1=kmb[b][:], op0=ALU.mult, op1=ALU.subtract), r=[r_ksum, r_km[b]], w=[r_km[b]])
        fns = []
        for t in range(8):
            fns.append(lambda e, t=t: e.matmul(ps[7][:, 256 + t * 32:256 + (t + 1) * 32], lhsT=qT[:, t * 128:(t + 1) * 128], rhs=kmb[b][:, 0:32], start=True, stop=False))
            fns.append(lambda e, t=t: e.matmul(ps[7][:, 256 + t * 32:256 + (t + 1) * 32], lhsT=qT[:, t * 128:(t + 1) * 128], rhs=kml[b][:, 0:32], start=False, stop=True))
        k.pe(fns, r=[rkv, r_km[b]], w=[r_ps[7]])
        k.op("dve", lambda e: e.tensor_tensor(out=gsm[b][:].rearrange("p (s t) n -> p s t n", s=4), in0=ps[7][:, 256:512].rearrange("p (s t n) -> p s t n", s=4, t=2),
                                              in1=pm_sb[:].rearrange("p (s n) -> p s n", s=4).unsqueeze(2).broadcast_to([128, 4, 2, 32]), op=ALU.add),
             r=[r_ps[7], r_pm], w=[r_sel[b]])
        for t in range(8):
            k.op("dve", lambda e, t=t: e.max(out=m8[b][:, t, :], in_=gsm[b][:, t, :]), r=[r_sel[b]], w=[r_sel[b]])
        k.op("dve", lambda e: e.tensor_scalar(out=m8[b][:, :, 2:3], in0=m8[b][:, :, 2:3], scalar1=-1.0e29, scalar2=None, op0=ALU.max), r=[r_sel[b]], w=[r_sel[b]])
        k.op("dve", lambda e: e.tensor_tensor(out=sel[b][:], in0=gsm[b][:], in1=m8[b][:, :, 2:3].broadcast_to([128, 8, 32]), op=ALU.is_ge), r=[r_sel[b]], w=[r_sel[b]])
        for s_ in range(4):
            k.op("dve", lambda e, s_=s_: e.tensor_tensor(out=stmp[b][:], in0=sel[b][:, 2 * s_, :], in1=oh_sb[:, s_ * 32:(s_ + 1) * 32], op=ALU.mult), r=[r_sel[b], r_pm], w=[r_sel[b]])
            k.op("dve", lambda e, s_=s_: e.tensor_reduce(out=selp[b][:, s_:s_ + 1], in_=stmp[b][:], axis=AX.X, op=ALU.add), r=[r_sel[b]], w=[r_sel[b]])

    head_loads(0)
    gating(0)
    sT_i = [0]
    tma_i = [0]
    for h in range(NH):
        b = h % 2
        kT, va, qT = kTb[b], vab[b], qTb[b]
        rkv = r_kv[b]
        selh, selph, r_selh = sel[b], selp[b], r_sel[b]
        if h + 1 < NH:
            head_loads(h + 1)
        k.op("pool", lambda e: e.memset(acc[:], 0.0), w=r_acc)

        blocks_ = [(s_, j) for s_ in range(4) for j in range(RS[s_])]
        SB = [0, 1, 6]

        def emit_qk(blk):
            s_, j = blk
            i_ = sT_i[0]
            sT_i[0] += 1
            sb_ = SB[i_ % 3]
            pt_ = i_ % 4
            k.pe([(lambda e, kt=kt: e.matmul(ps[sb_][:, kt * 256:(kt + 1) * 256], lhsT=kT[:, (2 * j + kt) * 128:(2 * j + kt + 1) * 128],
                                             rhs=qT[:, s_ * 256:(s_ + 1) * 256], start=True, stop=True)) for kt in range(2)],
                 r=[rkv], w=[r_ps[sb_]])
            k.op("act", lambda e: e.activation(out=PT[pt_][:], in_=ps[sb_][:, :], func=AF.Exp, scale=SCALE), r=[r_ps[sb_]], w=[r_PT[pt_]])
            return pt_

        def emit_pv(blk, pt_, n_):
            s_, j = blk
            ob = 2 + (n_ % 4)
            fns = []
            for qt in range(2):
                for kt in range(2):
                    fns.append(lambda e, qt=qt, kt=kt: e.matmul(ps[ob][:, qt * 256:qt * 256 + 129], lhsT=PT[pt_][:, kt * 256 + qt * 128:kt * 256 + (qt + 1) * 128],
                                                                rhs=va[:, 2 * j + kt, 0:129], start=(kt == 0), stop=(kt == 1)))
            k.pe(fns, r=[r_PT[pt_], rkv], w=[r_ps[ob]])
            use_act = ACT_SHARE > 0 and (n_ % ACT_SHARE) == ACT_SHARE - 1
            for qt in range(2):
                tl = s_ * 2 + qt
                src_ = ps[ob][:, qt * 256:qt * 256 + 129]
                if use_act:
                    x_ = tma_i[0] % 4
                    tma_i[0] += 1
                    k.op("act", lambda e: e.activation(out=tma[x_][:, 0, 0:129], in_=src_, func=AF.Copy, scale=selh[:, tl, j:j + 1]),
                         r=[r_ps[ob], r_selh], w=[r_tma[x_]])
                    k.op("pool", lambda e: e.tensor_tensor(out=acc[:, tl, 0:129], in0=acc[:, tl, 0:129], in1=tma[x_][:, 0, 0:129], op=ALU.add),
                         r=[r_tma[x_]], w=[r_acc[tl]])
                else:
                    k.op("dve", lambda e: e.scalar_tensor_tensor(out=acc[:, tl, 0:129], in0=src_, scalar=selh[:, tl, j:j + 1], in1=acc[:, tl, 0:129], op0=ALU.mult, op1=ALU.add),
                         r=[r_ps[ob], r_selh], w=[r_acc[tl]])

        def emit_qk_local(s):
            LTI = [16 * s + 13, 16 * s + 14, 16 * s + 15]
            Q0, Q1 = 2 * s, 2 * s + 1
            i_ = sT_i[0]
            sT_i[0] += 1
            sb_ = SB[i_ % 3]
            pt_ = i_ % 4

            def kt_(i):
                return kT[:, LTI[i] * 128:(LTI[i] + 1) * 128]

            def q_(i):
                return qT[:, i * 128:(i + 1) * 128]

            k.pe([lambda e: e.matmul(ps[sb_][:, 0:128], lhsT=kt_(1), rhs=q_(Q0), start=True, stop=False),
                  lambda e: e.matmul(ps[sb_][:, 0:128], lhsT=ident[:], rhs=BdT[:, h, :], start=False, stop=True),
                  lambda e: e.matmul(ps[sb_][:, 128:256], lhsT=kt_(1), rhs=q_(Q1), start=True, stop=False),
                  lambda e: e.matmul(ps[sb_][:, 128:256], lhsT=ident[:], rhs=BpT[:, h, :], start=False, stop=True),
                  lambda e: e.matmul(ps[sb_][:, 256:384], lhsT=kt_(2), rhs=q_(Q1), start=True, stop=False),
                  lambda e: e.matmul(ps[sb_][:, 256:384], lhsT=ident[:], rhs=BdT[:, h, :], start=False, stop=True),
                  lambda e: e.matmul(ps[sb_][:, 384:512], lhsT=kt_(0), rhs=q_(Q0), start=True, stop=True)],
                 r=[rkv, r_ident, r_bias], w=[r_ps[sb_]])
            k.op("act", lambda e: e.activation(out=PT[pt_][:], in_=ps[sb_][:, :], func=AF.Exp, scale=SCALE), r=[r_ps[sb_]], w=[r_PT[pt_]])
            k.op("dve", lambda e: e.tensor_tensor(out=PT[pt_][:, 384:512], in0=PT[pt_][:, 384:512], in1=Ecr[:, h, :], op=ALU.mult), r=[r_bias], w=[r_PT[pt_]])
            return pt_

        def emit_pv_local(s, pt_, n_):
            LTI = [16 * s + 13, 16 * s + 14, 16 * s + 15]
            Q0, Q1 = 2 * s, 2 * s + 1
            ob, ob2 = 2 + (n_ % 4), 2 + ((n_ + 1) % 4)
            PTl = PT[pt_]
            k.pe([lambda e: e.matmul(ps[ob][:, 0:129], lhsT=PTl[:, 0:128], rhs=va[:, LTI[1], 0:129], start=True, stop=True),
                  lambda e: e.matmul(ps[ob][:, 256:385], lhsT=PTl[:, 128:256], rhs=va[:, LTI[1], 0:129], start=True, stop=False),
                  lambda e: e.matmul(ps[ob][:, 256:385], lhsT=PTl[:, 256:384], rhs=va[:, LTI[2], 0:129], start=False, stop=True),
                  lambda e: e.matmul(ps[ob2][:, 0:129], lhsT=PTl[:, 384:512], rhs=va[:, LTI[0], 0:129], start=True, stop=True)],
                 r=[r_PT[pt_], rkv], w=[r_ps[ob], r_ps[ob2]])
            k.op("dve", lambda e: e.tensor_tensor(out=acc[:, Q0, 0:129], in0=acc[:, Q0, 0:129], in1=ps[ob][:, 0:129], op=ALU.add), r=[r_ps[ob]], w=[r_acc[Q0]])
            k.op("dve", lambda e: e.tensor_tensor(out=acc[:, Q1, 0:129], in0=acc[:, Q1, 0:129], in1=ps[ob][:, 256:385], op=ALU.add), r=[r_ps[ob]], w=[r_acc[Q1]])
            k.op("dve", lambda e: e.scalar_tensor_tensor(out=acc[:, Q0, 0:129], in0=ps[ob2][:, 0:129], scalar=selph[:, s:s + 1], in1=acc[:, Q0, 0:129], op0=ALU.mult, op1=ALU.add),
                 r=[r_ps[ob2], r_selh], w=[r_acc[Q0]])

        items = [("l", s_) for s_ in range(4)] + [("p", blk) for blk in blocks_]
        pend = []
        n_ = 0
        for ii, (kind, arg) in enumerate(items):
            if kind == "p":
                pend.append((kind, arg, emit_qk(arg), n_))
                n_ += 1
            else:
                pend.append((kind, arg, emit_qk_local(arg), n_))
                n_ += 2
            if len(pend) > PD:
                kd, a_, pt_, nn = pend.pop(0)
                (emit_pv if kd == "p" else emit_pv_local)(a_, pt_, nn)
            if ii == 12 and h + 1 < NH:
                gating(h + 1)
        while pend:
            kd, a_, pt_, nn = pend.pop(0)
            (emit_pv if kd == "p" else emit_pv_local)(a_, pt_, nn)

        k.op("dve", lambda e: e.reciprocal(out=rec[:], in_=acc[:, :, 128]), r=r_acc, w=[r_rec])
        k.op("dve", lambda e: e.tensor_tensor(out=obf[:], in0=acc[:, :, 0:128], in1=rec[:].unsqueeze(2).broadcast_to([128, 8, 128]), op=ALU.mult), r=r_acc + [r_rec], w=[r_obf])
        pb = ps[7][:].bitcast(BF16)
        for hf in range(2):
            k.pe([(lambda e, t=t: e.transpose(out=pb[:, (t % 4) * 128:(t % 4 + 1) * 128], in_=obf[:, t, :], identity=ident[:])) for t in range(hf * 4, hf * 4 + 4)],
                 r=[r_obf, r_ident], w=[r_ps[7]])
            k.op("act", lambda e: e.activation(out=attnT[:, h, hf * 512:(hf + 1) * 512], in_=pb[:, 0:512], func=AF.Copy), r=[r_ps[7]], w=[r_att])
    if dbg:
        k.dma("sp", att_d[:, :], attnT[:].rearrange("p c t -> p (c t)"), r=[r_att])
    if stop_after == "C":
        k.barrier()
        return nc

    k.barrier()
    off = offC0
    mergedT, off = T("mergedT", [128, 16, TOK], BF16, off)
    r_mer = k.res()
    offD1 = off
    hTo, off = T("hToD", [128, 16, TOK], BF16, off)
    mixT, off = T("mixTD", [128, 16, TOK], BF16, off)
    r_ho, r_mi = k.res(), k.res()
    wu = [None, None]
    wu[0], off = T("wu0", [128, 16, 4, 256], BF16, off)
    wu[1], off = T("wu1", [128, 16, 4, 256], BF16, off)
    r_wu = [k.res(), k.res()]
    sa, off = T("sa", [128, 512], F32, off)
    sbb, off = T("sb", [128, 512], F32, off)
    r_sa, r_sb = k.res(), k.res()
    assert off <= SB_LIMIT, off
    k.dma("sp", hTo[:].rearrange("p c t -> p (c t)"), hTo_d[:, :], w=[r_ho])
    k.dma("sp", mixT[:].rearrange("p c t -> p (c t)"), mix_d[:, :], w=[r_mi])
    it = 0
    for jj in range(8):
        b = jj % 2
        for i, view in enumerate((w_pa_v[:, :, jj * 256:(jj + 1) * 256], w_pb_v[:, :, jj * 256:(jj + 1) * 256],
                                  w_in_v[:, :, 10240 + jj * 256:10240 + (jj + 1) * 256], w_in_v[:, :, 12288 + jj * 256:12288 + (jj + 1) * 256])):
            k.dma("pool", wu[b][:, :, i, :], view, w=[r_wu[b]])
        for jh in range(2):
            j = jj * 2 + jh
            for half in range(2):
                bs = 4 * (it % 2)
                it += 1
                a0 = half * 512
                for i, (act_, r_act) in enumerate(((mixT, r_mi), (attnT, r_att), (hTo, r_ho), (hTo, r_ho))):
                    k.pe([(lambda e, c=c, i=i, act_=act_: e.matmul(ps[bs + i][:, 0:512], lhsT=wu[b][:, c, i, jh * 128:(jh + 1) * 128], rhs=act_[:, c, a0:a0 + 512], start=(c == 0), stop=(c == 15)))
                          for c in range(16)], r=[r_wu[b], r_act], w=[r_ps[bs + i]])
                k.op("act", lambda e: e.activation(out=sa[:], in_=ps[bs + 2][:, :], func=AF.Sigmoid), r=[r_ps[bs + 2]], w=[r_sa])
                k.op("act", lambda e: e.activation(out=sbb[:], in_=ps[bs + 3][:, :], func=AF.Sigmoid), r=[r_ps[bs + 3]], w=[r_sb])
                k.op("dve", lambda e: e.tensor_tensor(out=sa[:], in0=sa[:], in1=ps[bs + 0][:, :], op=ALU.mult), r=[r_ps[bs + 0]], w=[r_sa])
                k.op("dve", lambda e: e.tensor_tensor(out=sbb[:], in0=sbb[:], in1=ps[bs + 1][:, :], op=ALU.mult), r=[r_ps[bs + 1]], w=[r_sb])
                k.op("dve", lambda e: e.tensor_tensor(out=mergedT[:, j, a0:a0 + 512], in0=sa[:], in1=sbb[:], op=ALU.add), r=[r_sa, r_sb], w=[r_mer])
    k.barrier()
    off = offD1
    x2, off = T("x2", [128, 8, D], F32, off)
    r_x2 = [k.res() for _ in range(8)]
    offX = off
    slabs = [None, None]
    slabs[0], off = T("slabD0", [128, 16, 512], BF16, off)
    slabs[1], off = T("slabD1", [128, 16, 512], BF16, off)
    r_slabs = [k.res(), k.res()]
    offS = off
    assert off <= SB_LIMIT, off
    for t in range(8):
        k.dma("sp", x2[:, t, :], xo[t * 128:(t + 1) * 128, :], w=[r_x2[t]])
    for sl in range(4):
        slab, r_sl = load_slab(slabs, r_slabs, w_out_v[:, :, sl * 512:(sl + 1) * 512])
        for t in range(8):
            bk = next_bank(8)
            mm_tm(bk, mergedT, t * 128, slab, 512, r=[r_sl, r_mer])
            k.op("dve", lambda e: e.tensor_tensor(out=x2[:, t, sl * 512:(sl + 1) * 512], in0=x2[:, t, sl * 512:(sl + 1) * 512], in1=ps[bk][:, :], op=ALU.add),
                 r=[r_ps[bk]], w=[r_x2[t]])
    if dbg:
        k.dma("sp", x2_d[:, :], x2[:].rearrange("p t d -> p (t d)"), r=r_x2)
    if stop_after == "D":
        k.barrier()
        return nc

    pre_slab = load_slab(slabs, r_slabs, w_up_v[:, :, 0:512])
    k.barrier()
    k.dma("sp", gbc[:], g_mlp[:, :], w=[r_gbc])
    off = offC0
    h2T, off = T("h2T", [128, 16, TOK], BF16, off)
    r_h2 = [k.res() for _ in range(8)]
    assert off <= offD1
    off = offS
    upT, off = T("upT", [128, 16, TOK], BF16, off)
    r_up = k.res()
    assert off <= SB_LIMIT, off
    off = P0
    xn[0], off = T("xnE0", [128, D], BF16, off)
    xn[1], off = T("xnE1", [128, D], BF16, off)
    junk, off = T("junkE", [128, D], BF16, off)
    rl = [None, None]
    rl[0], off = T("rl0", [128, 512], F32, off)
    rl[1], off = T("rl1", [128, 512], F32, off)
    r_rl = [k.res(), k.res()]
    assert off <= offC0, off
    tmpE = ([None, None], [None, None], xn, [k.res(), k.res()], junk, k.res())

    def src_x2(t, xb, rxb):
        return x2[:, t, :], r_x2[t]

    prep_hT(src_x2, 8, h2T, r_h2, tmpE)
    ri = 0
    for fg in range(4):
        for sl in range(4):
            if fg == 0 and sl == 0:
                slab, r_sl = pre_slab
            else:
                slab, r_sl = load_slab(slabs, r_slabs, w_up_v[:, :, (fg * 4 + sl) * 512:(fg * 4 + sl + 1) * 512])
            for fc in range(4):
                for half in range(2):
                    bk = next_bank()
                    mm_fm(bk, 512, slab, fc * 128, h2T, half * 512, r=[r_sl] + r_h2[half * 4:half * 4 + 4])
                    i = ri % 2
                    ri += 1
                    k.op("act", lambda e: e.activation(out=rl[i][:], in_=ps[bk][:, :], func=AF.Relu), r=[r_ps[bk]], w=[r_rl[i]])
                    k.op("dve", lambda e: e.tensor_tensor(out=upT[:, sl * 4 + fc, half * 512:(half + 1) * 512], in0=rl[i][:], in1=rl[i][:], op=ALU.mult), r=[r_rl[i]], w=[r_up])
        wd_v = w_down[fg * 2048:(fg + 1) * 2048, :].rearrange("(c p) n -> p c n", p=128)
        for sl in range(4):
            slab, r_sl = load_slab(slabs, r_slabs, wd_v[:, :, sl * 512:(sl + 1) * 512])
            for t in range(8):
                bk = next_bank()
                mm_tm(bk, upT, t * 128, slab, 512, r=[r_sl, r_up])
                k.op("dve", lambda e: e.tensor_tensor(out=x2[:, t, sl * 512:(sl + 1) * 512], in0=x2[:, t, sl * 512:(sl + 1) * 512], in1=ps[bk][:, :], op=ALU.add),
                     r=[r_ps[bk]], w=[r_x2[t]])

    gfin, _ = T("gfin", [128, D], F32, P0)
    r_gfin = k.res()
    k.dma("sp", gfin[:], g_fin[:, :], w=[r_gfin, tmpE[3][0], tmpE[3][1]])
    off = offC0
    ot = [None, None]
    ot[0], off = T("ot0", [128, D], F32, off)
    ot[1], off = T("ot1", [128, D], F32, off)
    r_ot = [k.res(), k.res()]
    for t in range(8):
        st, r_st = next_stat()
        k.op("dve", lambda e: e.memset(st[:, 0:2], 0.0), w=[r_st])
        k.op("act", lambda e: e.activation(out=junk[:], in_=x2[:, t, :], func=AF.Square, accum_out=st[:, 0:1]), r=[r_x2[t]], w=[tmpE[5], r_st])
        rstd_from_ss(st[:, 0:1], st[:, 1:2], r_st, D)
        b = t % 2
        k.op("dve", lambda e: e.scalar_tensor_tensor(out=ot[b][:], in0=x2[:, t, :], scalar=st[:, 1:2], in1=gfin[:], op0=ALU.mult, op1=ALU.mult),
             r=[r_x2[t], r_st, r_gfin], w=[r_ot[b]] + (r_h2 if t < 2 else []))
        k.dma("sp", out[t * 128:(t + 1) * 128, :], ot[b][:], r=[r_ot[b]])
    k.barrier()
    return nc


def _t5_bucket(n):
    n = np.maximum(n, 0)
    nf = np.maximum(n, 1).astype(np.float32)
    large = 16 + (np.log(nf / np.float32(16)) / np.float32(math.log(128 / 16)) * np.float32(16)).astype(np.int32)
    large = np.minimum(large, 31)
    return np.where(n < 16, n, large)


def core_blocks(c):
    return [c, 8 + c, 16 + c, 24 + c]


def make_inputs(c, x, ln_mix, w_in, a_v_gain, a_spatial, a_spatial_bias, w_proj_a, w_proj_b,
                w_out, rel_bias, ln_mlp, w_up, w_down, ln_final, shared=None):
    x2d = np.asarray(x, np.float32).reshape(S, D)
    blocks = core_blocks(c)
    if shared is None:
        shared = {}
        rep = lambda v: np.ascontiguousarray(np.broadcast_to(np.asarray(v, np.float32).reshape(1, -1), (128, np.asarray(v).size)))
        shared["w_in"] = np.ascontiguousarray(np.asarray(w_in, np.float32)[0])
        shared["w_pa"] = np.ascontiguousarray(np.asarray(w_proj_a, np.float32)[0])
        shared["w_pb"] = np.ascontiguousarray(np.asarray(w_proj_b, np.float32)[0])
        shared["w_out"] = np.ascontiguousarray(np.asarray(w_out, np.float32)[0])
        shared["w_up"] = np.ascontiguousarray(np.asarray(w_up, np.float32)[0])
        shared["w_down"] = np.ascontiguousarray(np.asarray(w_down, np.float32)[0])
        shared["g_mix"] = rep(np.asarray(ln_mix)[0])
        shared["g_v"] = rep(np.asarray(a_v_gain)[0])
        shared["g_mlp"] = rep(np.asarray(ln_mlp)[0])
        shared["g_fin"] = rep(np.asarray(ln_final))
        asp = np.asarray(a_spatial, np.float32)[0]
        shared["wsT"] = np.ascontiguousarray(asp.transpose(2, 0, 1).reshape(128, 16 * 128))
        s_i = np.arange(128)[:, None]
        t_i = np.arange(128)[None, :]
        shared["cmask"] = (s_i <= t_i).astype(np.float32)
        bs = np.asarray(a_spatial_bias, np.float32)[0]
        shared["bsb"] = np.ascontiguousarray(np.broadcast_to(bs.reshape(1, 16 * 128), (128, 16 * 128)))
        rb = np.asarray(rel_bias, np.float32)
        kk = np.arange(128)[:, None]
        qq = np.arange(128)[None, :]
        bd = _t5_bucket(qq - kk)
        td = rb[bd, :]
        td = np.where((qq >= kk)[:, :, None], td, rb[31][None, None, :])
        shared["tdg"] = np.ascontiguousarray(td.transpose(0, 2, 1).reshape(128, 16 * 128))
        bp = _t5_bucket(qq - kk + 128)
        tp = rb[bp, :]
        shared["tpv"] = np.ascontiguousarray(tp.transpose(0, 2, 1).reshape(128, 16 * 128))
        shared["cfar"] = np.ascontiguousarray(np.broadcast_to(rb[31].reshape(1, 16), (128, 16)))
        shared["negm"] = np.where(qq >= kk, 0.0, NEGV).astype(np.float32)
        shared["idn"] = np.eye(128, dtype=np.float32).astype(ml_dtypes.bfloat16)
    m = dict(shared)
    vorder = [None] * 32
    fixed = set()
    for s in range(4):
        vorder[8 * s + 7] = blocks[s]
        fixed.add(blocks[s])
        if blocks[s] > 0:
            vorder[8 * s + 6] = blocks[s] - 1
            fixed.add(blocks[s] - 1)
    rest = [r for r in range(32) if r not in fixed]
    for p in range(32):
        if vorder[p] is None:
            vorder[p] = rest.pop(0)
    assert sorted(vorder) == list(range(32))
    for s in range(4):
        assert all(vorder.index(r) < 8 * s + 7 for r in range(blocks[s]))
    rows = [x2d[r * LB:(r + 1) * LB] for r in vorder]
    m["xa"] = np.ascontiguousarray(np.concatenate(rows, 0))
    m["xo"] = np.ascontiguousarray(np.concatenate([x2d[b * LB:(b + 1) * LB] for b in blocks], 0))
    pm = np.zeros((4, 32), np.float32)
    oh = np.zeros((4, 32), np.float32)
    for s, b in enumerate(blocks):
        for v, r in enumerate(vorder):
            if r >= b:
                pm[s, v] = -1.0e30
            if r == b - 1:
                oh[s, v] = 1.0
    m["pm"] = np.ascontiguousarray(np.broadcast_to(pm.reshape(1, 128), (128, 128)))
    m["oh"] = np.ascontiguousarray(np.broadcast_to(oh.reshape(1, 128), (128, 128)))
    return m, shared


_NC_CACHE = {}


def kernel(**inputs):
    if "nc" not in _NC_CACHE:
        _NC_CACHE["nc"] = build()
    nc = _NC_CACHE["nc"]
    in_maps = []
    shared = None
    for c in range(8):
        m, shared = make_inputs(c, shared=shared, **inputs)
        in_maps.append(m)
    res = run_bass_kernel_spmd(nc, in_maps, core_ids=list(range(8)))
    outp = np.zeros((S, D), np.float32)
    for c in range(8):
        o = np.asarray(res.results[c]["out"], np.float32)
        for s, b in enumerate(core_blocks(c)):
            outp[b * LB:(b + 1) * LB] = o[s * LB:(s + 1) * LB]
    return outp.reshape(1, S, D)
```

```python
import os
import math
import numpy as np
import ml_dtypes
import concourse.bass as bass
import concourse.mybir as mybir
from concourse.bass_utils import run_bass_kernel_spmd

F32 = mybir.dt.float32
BF16 = mybir.dt.bfloat16
AF = mybir.ActivationFunctionType
ALU = mybir.AluOpType
AX = mybir.AxisListType

D = 2048
S = 8192
NH = 16
HD = 128
LB = 256
NBLK = 32
FF = 8192
INC = 14336
TOK = 1024
NPAST = 31 * LB
NVT = S
EPS = 1e-6
SCALE = HD ** -0.5
NEGV = -1.0e5
GROUPS = [(i * 8, 8) for i in range(8)]
RS = [7, 15, 23, 31]
ACT_SHARE = int(os.environ.get('KACT_SHARE', '0'))
PD = int(os.environ.get('KPD', '3'))


class Ev:
    __slots__ = ("key", "sem", "val")

    def __init__(self, key, sem, val):
        self.key = key
        self.sem = sem
        self.val = val


class Res:
    __slots__ = ("name", "w", "r")

    def __init__(self, name):
        self.name = name
        self.w = None
        self.r = {}


class Eng:
    def __init__(self, name, obj):
        self.name = name
        self.obj = obj
        self.sem = None
        self.cnt = 0
        self.epoch = 0
        self.seen = {}
        self.last = None


class K:
    EPOCH = 30000
    NDMA = 16

    def __init__(self, nc):
        self.nc = nc
        self.eng = {
            "pe": Eng("pe", nc.tensor),
            "act": Eng("act", nc.scalar),
            "dve": Eng("dve", nc.vector),
            "pool": Eng("pool", nc.gpsimd),
            "sp": Eng("sp", nc.sync),
        }
        for e in self.eng.values():
            e.sem = nc.alloc_semaphore(f"s_{e.name}_0")
        self.dq = {}
        for qn in ("sp", "pool"):
            self.dq[qn] = {"sem": [nc.alloc_semaphore(f"s_dma_{qn}_{i}") for i in range(self.NDMA)], "cnt": [0] * self.NDMA, "next": 0}
        self.nres = 0

    def res(self, name=None):
        self.nres += 1
        return Res(name or f"r{self.nres}")

    def _wait(self, e, ev):
        if ev is None:
            return
        if e.name == "pe" and ev.key[0] == "pe":
            return
        if e.seen.get(ev.key, 0) >= ev.val:
            return
        e.obj.wait_ge(ev.sem, ev.val)
        e.seen[ev.key] = ev.val

    def _deps(self, e, r, w):
        for x in r:
            self._wait(e, x.w)
        for x in w:
            self._wait(e, x.w)
            for ev in x.r.values():
                self._wait(e, ev)

    def _mark(self, ev, r, w):
        for x in r:
            x.r[ev.key] = ev
        for x in w:
            x.w = ev
            x.r = {}

    def _signal(self, e, ins):
        if e.cnt >= self.EPOCH:
            e.epoch += 1
            e.cnt = 0
            e.sem = self.nc.alloc_semaphore(f"s_{e.name}_{e.epoch}")
        e.cnt += 1
        ins.then_inc(e.sem, 1)
        ev = Ev((e.name, e.epoch), e.sem, e.cnt)
        e.last = ev
        return ev

    def op(self, en, fn, r=(), w=()):
        e = self.eng[en]
        self._deps(e, r, w)
        ins = fn(e.obj)
        ev = self._signal(e, ins)
        self._mark(ev, r, w)
        return ev

    def pe(self, fns, r=(), w=()):
        e = self.eng["pe"]
        self._deps(e, r, w)
        ins = None
        for fn in fns:
            ins = fn(e.obj)
        ev = self._signal(e, ins)
        self._mark(ev, r, w)
        return ev

    def dma(self, qn, out, in_, r=(), w=()):
        e = self.eng[qn]
        q = self.dq[qn]
        self._deps(e, r, w)
        i = q["next"]
        q["next"] = (i + 1) % self.NDMA
        if q["cnt"][i] > 0:
            self._wait(e, Ev(("dma", qn, i), q["sem"][i], q["cnt"][i] * 16))
        q["cnt"][i] += 1
        e.obj.dma_start(out=out, in_=in_).then_inc(q["sem"][i], 16)
        ev = Ev(("dma", qn, i), q["sem"][i], q["cnt"][i] * 16)
        self._mark(ev, r, w)
        return ev

    def _wait_any(self, e, ev):
        if e.seen.get(ev.key, 0) >= ev.val:
            return
        e.obj.wait_ge(ev.sem, ev.val)
        e.seen[ev.key] = ev.val

    def barrier(self):
        evs = [e.last for e in self.eng.values() if e.last is not None]
        for qn, q in self.dq.items():
            for i in range(self.NDMA):
                if q["cnt"][i] > 0:
                    evs.append(Ev(("dma", qn, i), q["sem"][i], q["cnt"][i] * 16))
        for e in self.eng.values():
            for ev in evs:
                self._wait_any(e, ev)


def build(stop_after=None, dbg=False):
    nc = bass.Bass("TRN2", target_bir_lowering=False)
    k = K(nc)

    def din(name, shape, dt=F32):
        return nc.dram_tensor(name, shape, dt, kind="ExternalInput").ap()

    xa = din("xa", [NVT, D])
    xo = din("xo", [TOK, D])
    w_in = din("w_in", [D, INC])
    w_pa = din("w_pa", [D, D])
    w_pb = din("w_pb", [D, D])
    w_out = din("w_out", [D, D])
    w_up = din("w_up", [D, FF])
    w_down = din("w_down", [FF, D])
    g_mix = din("g_mix", [128, D])
    g_v = din("g_v", [128, D])
    g_mlp = din("g_mlp", [128, D])
    g_fin = din("g_fin", [128, D])
    wsT = din("wsT", [128, 16 * 128])
    cmask = din("cmask", [128, 128])
    bsb = din("bsb", [128, 16 * 128])
    tdg = din("tdg", [128, 16 * 128])
    tpv = din("tpv", [128, 16 * 128])
    cfar = din("cfar", [128, 16])
    negm = din("negm", [128, 128])
    pmd = din("pm", [128, 128])
    ohd = din("oh", [128, 128])
    idn = din("idn", [128, 128], BF16)
    out = nc.dram_tensor("out", [TOK, D], F32, kind="ExternalOutput").ap()

    skind = "ExternalOutput" if dbg else "Internal"
    kT_d = nc.dram_tensor("kT_d", [NH, 128, NVT], BF16, kind=skind).ap()
    v_d = nc.dram_tensor("v_d", [NVT, D], BF16, kind=skind).ap()
    qT_d = nc.dram_tensor("qT_d", [NH, 128, TOK], BF16, kind=skind).ap()
    hTo_d = nc.dram_tensor("hTo_d", [128, 16 * TOK], BF16, kind=skind).ap()
    mix_d = nc.dram_tensor("mix_d", [128, 16 * TOK], BF16, kind=skind).ap()
    ksum_d = nc.dram_tensor("ksum_d", [128, 16 * 40], F32, kind=skind).ap()
    att_d = nc.dram_tensor("att_d", [128, 16 * TOK], BF16, kind=skind).ap() if dbg else None
    x2_d = nc.dram_tensor("x2_d", [128, 8 * D], F32, kind=skind).ap() if dbg else None

    w_in_v = w_in.rearrange("(c p) n -> p c n", p=128)
    w_pa_v = w_pa.rearrange("(c p) n -> p c n", p=128)
    w_pb_v = w_pb.rearrange("(c p) n -> p c n", p=128)
    w_out_v = w_out.rearrange("(c p) n -> p c n", p=128)
    w_up_v = w_up.rearrange("(c p) n -> p c n", p=128)

    uid = [0]

    def T(name, shape, dt, off):
        uid[0] += 1
        n = 1
        for s_ in shape[1:]:
            n *= s_
        nbytes = n * (4 if dt == F32 else 2)
        t = nc.alloc_sbuf_tensor_at(f"{name}_{uid[0]}", shape, dt, offset=off)
        return t, off + ((nbytes + 63) // 64) * 64

    ps = [nc.alloc_psum_tensor(f"ps{i}", [128, 512], F32) for i in range(8)]
    r_ps = [k.res(f"ps{i}") for i in range(8)]

    off = 16512
    ident, off = T("ident", [128, 128], BF16, off)
    stat, off = T("stat", [128, 64], F32, off)
    gbc, off = T("gbc", [128, D], F32, off)
    r_ident, r_gbc = k.res(), k.res()
    ksum, off = T("ksum", [128, 16, 40], F32, off)
    r_ksum = k.res()
    P0 = off
    SB_LIMIT = 229280

    k.dma("sp", ident[:], idn[:, :], w=[r_ident])
    k.dma("sp", gbc[:], g_mix[:, :], w=[r_gbc])
    k.op("dve", lambda e: e.memset(ksum[:], 0.0), w=[r_ksum])

    stat_slots = [k.res() for _ in range(8)]
    stat_i = [0]

    def next_stat():
        i = stat_i[0] % 8
        stat_i[0] += 1
        return stat[:, 8 * i:8 * i + 8], stat_slots[i]

    bank_i = [0]

    def next_bank(n=6):
        b = bank_i[0] % n
        bank_i[0] += 1
        return b

    evac_i = [0]

    def evac_copy(dst, src, r, w):
        evac_i[0] += 1
        if evac_i[0] % 2:
            return k.op("act", lambda e: e.activation(out=dst, in_=src, func=AF.Copy), r=r, w=w)
        return k.op("dve", lambda e: e.tensor_copy(out=dst, in_=src), r=r, w=w)

    def rstd_from_ss(ss_ap, out_ap, r_st, n):
        k.op("dve", lambda e: e.tensor_scalar(out=out_ap, in0=ss_ap, scalar1=1.0 / n, scalar2=EPS, op0=ALU.mult, op1=ALU.add), r=[r_st], w=[r_st])
        k.op("act", lambda e: e.activation(out=out_ap, in_=out_ap, func=AF.Sqrt), r=[r_st], w=[r_st])
        k.op("dve", lambda e: e.reciprocal(out=out_ap, in_=out_ap), r=[r_st], w=[r_st])

    def prep_load(src_fn, t, tmp):
        xbuf, r_xb = tmp[0], tmp[1]
        return src_fn(t, xbuf[t % 2], r_xb[t % 2])

    def prep_norm(loaded, t, tmp):
        xbuf, r_xb, xn, r_xn, junk, r_junk = tmp
        b = t % 2
        xin, r_in = loaded
        st, r_st = next_stat()
        k.op("dve", lambda e: e.memset(st[:, 0:2], 0.0), w=[r_st])
        k.op("act", lambda e: e.activation(out=junk[:], in_=xin, func=AF.Square, accum_out=st[:, 0:1]), r=[r_in], w=[r_junk, r_st])
        rstd_from_ss(st[:, 0:1], st[:, 1:2], r_st, D)
        k.op("dve", lambda e: e.scalar_tensor_tensor(out=xn[b][:], in0=xin, scalar=st[:, 1:2], in1=gbc[:], op0=ALU.mult, op1=ALU.mult),
             r=[r_in, r_st, r_gbc], w=[r_xn[b]])

    def prep_tr(t, hT, r_hT_tiles, tmp):
        xbuf, r_xb, xn, r_xn, junk, r_junk = tmp
        b = t % 2
        for half in range(2):
            pb = ps[6 + half][:].bitcast(BF16)
            k.pe([(lambda e, c=c: e.transpose(out=pb[:, (c % 8) * 128:(c % 8 + 1) * 128], in_=xn[b][:, c * 128:(c + 1) * 128], identity=ident[:]))
                  for c in range(half * 8, half * 8 + 8)], r=[r_xn[b], r_ident], w=[r_ps[6 + half]])
            src = pb.rearrange("p (c t) -> p c t", c=8)
            dst = hT[:, half * 8:half * 8 + 8, t * 128:(t + 1) * 128]
            if half == 0:
                k.op("act", lambda e: e.activation(out=dst, in_=src, func=AF.Copy), r=[r_ps[6]], w=[r_hT_tiles[t]])
            else:
                k.op("dve", lambda e: e.tensor_copy(out=dst, in_=src), r=[r_ps[7]], w=[r_hT_tiles[t]])

    def prep_tile(src_fn, t, hT, r_hT_tiles, tmp):
        ld = prep_load(src_fn, t, tmp)
        prep_norm(ld, t, tmp)
        prep_tr(t, hT, r_hT_tiles, tmp)

    def prep_hT(src_fn, ntiles, hT, r_hT_tiles, tmp):
        prep_norm(prep_load(src_fn, 0, tmp), 0, tmp)
        for t in range(ntiles):
            if t + 1 < ntiles:
                prep_norm(prep_load(src_fn, t + 1, tmp), t + 1, tmp)
            prep_tr(t, hT, r_hT_tiles, tmp)

    def mm_fm(bank, N, W, wc0, act, a0, r, nchunk=16):
        k.pe([(lambda e, c=c: e.matmul(ps[bank][:, 0:N], lhsT=W[:, c, wc0:wc0 + 128], rhs=act[:, c, a0:a0 + N], start=(c == 0), stop=(c == nchunk - 1)))
              for c in range(nchunk)], r=r, w=[r_ps[bank]])

    def mm_tm(bank, act, a0, W, N, r, nchunk=16):
        k.pe([(lambda e, c=c: e.matmul(ps[bank][:, 0:N], lhsT=act[:, c, a0:a0 + 128], rhs=W[:, c, 0:N], start=(c == 0), stop=(c == nchunk - 1)))
              for c in range(nchunk)], r=r, w=[r_ps[bank]])

    slab_i = [0]

    def load_slab(slabs, r_slabs, view):
        b = slab_i[0] % 2
        slab_i[0] += 1
        k.dma("pool", slabs[b][:], view, w=[r_slabs[b]])
        return slabs[b], r_slabs[b]

    off = P0
    hTb = [None, None]
    hTb[0], off = T("hT0", [128, 16, TOK], BF16, off)
    hTb[1], off = T("hT1", [128, 16, TOK], BF16, off)
    r_hT = [[k.res() for _ in range(8)] for _ in range(2)]
    xbuf = [None, None]
    xbuf[0], off = T("xb0", [128, D], F32, off)
    xbuf[1], off = T("xb1", [128, D], F32, off)
    r_xb = [k.res(), k.res()]
    xn = [None, None]
    xn[0], off = T("xn0", [128, D], BF16, off)
    xn[1], off = T("xn1", [128, D], BF16, off)
    r_xn = [k.res(), k.res()]
    junk, off = T("junk", [128, D], BF16, off)
    r_junk = k.res()
    tmpA = (xbuf, r_xb, xn, r_xn, junk, r_junk)
    slabs = [None, None]
    slabs[0], off = T("slab0", [128, 16, 512], BF16, off)
    slabs[1], off = T("slab1", [128, 16, 512], BF16, off)
    r_slabs = [k.res(), k.res()]
    offA = off
    kst = [None, None]
    kst[0], off = T("kst0", [128, 4, TOK], BF16, off)
    kst[1], off = T("kst1", [128, 4, TOK], BF16, off)
    r_kst = [k.res(), k.res()]
    vst = [None, None]
    vst[0], off = T("vst0", [128, 8, 512], BF16, off)
    vst[1], off = T("vst1", [128, 8, 512], BF16, off)
    r_vst = [k.res(), k.res()]
    assert off <= SB_LIMIT, off

    kT_dv = kT_d.rearrange("h d t -> d h t")
    st_i = 0
    def src_a_of(t0):
        def src_a(t, xb, rxb):
            k.dma("sp", xb[:], xa[(t0 + t) * 128:(t0 + t + 1) * 128, :], w=[rxb])
            return xb[:], rxb
        return src_a

    def src_o(t, xb, rxb):
        k.dma("sp", xb[:], xo[t * 128:(t + 1) * 128, :], w=[rxb])
        return xb[:], rxb

    NG = len(GROUPS)
    prep_hT(src_a_of(0), GROUPS[0][1], hTb[NG % 2], r_hT[NG % 2], tmpA)
    for gi, (t0, nt) in enumerate(GROUPS):
        hT = hTb[(gi + NG) % 2]
        rh = r_hT[(gi + NG) % 2]
        NT = nt * 128
        if gi + 1 < len(GROUPS):
            nxt = (src_a_of(GROUPS[gi + 1][0]), GROUPS[gi + 1][1], hTb[(gi + 1 + NG) % 2], r_hT[(gi + 1 + NG) % 2])
        else:
            nxt = (src_o, 8, hTb[(gi + 1 + NG) % 2], r_hT[(gi + 1 + NG) % 2])
        if gi == 0:
            pending = prep_load(nxt[0], 0, tmpA)
        for sl in range(8):
            do_prep = sl < nxt[1]
            if do_prep:
                prep_norm(pending, sl, tmpA)
            if sl + 1 < nxt[1]:
                pending = prep_load(nxt[0], sl + 1, tmpA)
            slab, r_sl = load_slab(slabs, r_slabs, w_in_v[:, :, 6144 + sl * 512:6144 + (sl + 1) * 512])
            sb = st_i % 2
            st_i += 1
            if sl < 4:
                for hh in range(4):
                    for a0 in range(0, NT, 512):
                        N = min(512, NT - a0)
                        bk = next_bank()
                        mm_fm(bk, N, slab, hh * 128, hT, a0, r=[r_sl] + rh[a0 // 128:(a0 + N) // 128])
                        for bo in range(0, N, LB):
                            vb = (t0 * 128 + a0 + bo) // LB
                            k.op("act", lambda e, bo=bo, vb=vb: e.activation(out=kst[sb][:, hh, a0 + bo:a0 + bo + LB], in_=ps[bk][:, bo:bo + LB], func=AF.Copy,
                                                                             accum_out=ksum[:, sl * 4 + hh, vb:vb + 1]),
                                 r=[r_ps[bk]], w=[r_kst[sb], r_ksum])
                k.dma("sp", kT_dv[:, sl * 4:(sl + 1) * 4, t0 * 128:t0 * 128 + NT], kst[sb][:, :, 0:NT], r=[r_kst[sb]])
            else:
                for t in range(nt):
                    bk = next_bank()
                    mm_tm(bk, hT, t * 128, slab, 512, r=[r_sl, rh[t]])
                    k.op("dve", lambda e: e.tensor_copy(out=vst[sb][:, t, :], in_=ps[bk][:, :]), r=[r_ps[bk]], w=[r_vst[sb]])
                k.dma("sp", v_d[t0 * 128:t0 * 128 + NT, (sl - 4) * 512:(sl - 3) * 512].rearrange("(n p) c -> p n c", p=128),
                      vst[sb][:, 0:nt, :], r=[r_vst[sb]])
            if do_prep:
                prep_tr(sl, nxt[2], nxt[3], tmpA)
            if sl == 7 and gi + 2 < len(GROUPS) + 1:
                if gi + 2 < len(GROUPS):
                    pending = prep_load(src_a_of(GROUPS[gi + 2][0]), 0, tmpA)
                else:
                    pending = prep_load(src_o, 0, tmpA)
    OWN_BUF = (2 * NG) % 2
    k.dma("sp", ksum_d[:, :], ksum[:].rearrange("p h n -> p (h n)"), r=[r_ksum])

    if stop_after == "A":
        k.barrier()
        return nc

    k.barrier()
    assert OWN_BUF == 0
    hTo = hTb[0]
    rho = r_hT[0]
    off = P0 + 32 * 1024
    vg, off = T("vg", [128, 8, D], F32, off)
    r_vg = [k.res() for _ in range(8)]
    offB1 = off
    vln, off = T("vln", [128, 8, D], BF16, off)
    r_vln = [k.res() for _ in range(8)]
    slabs = [None, None]
    slabs[0], off = T("slabB0", [128, 16, 512], BF16, off)
    slabs[1], off = T("slabB1", [128, 16, 512], BF16, off)
    r_slabs = [k.res(), k.res()]
    vgbc, off = T("vgbc", [128, D], F32, off)
    r_vgbc = k.res()
    wTm, off = T("wTm", [128, 16, 128], BF16, off)
    r_wTm = k.res()
    bbc, off = T("bbc", [128, 16 * 128], F32, off)
    r_bbc = k.res()
    st6, off = T("st6", [128, 4, 6], F32, off)
    r_st6 = k.res()
    qst = [None, None]
    qst[0], off = T("qst0", [128, 4, TOK], BF16, off)
    qst[1], off = T("qst1", [128, 4, TOK], BF16, off)
    r_qst = [k.res(), k.res()]
    assert off <= SB_LIMIT, off
    k.dma("sp", vgbc[:], g_v[:, :], w=[r_vgbc])
    k.dma("sp", bbc[:], bsb[:, :], w=[r_bbc])
    wtmp = vg[:, 0, :]
    k.dma("sp", wtmp, wsT[:, :], w=[r_vg[0]])
    cm = vg[:, 1, 0:128]
    k.dma("sp", cm, cmask[:, :], w=[r_vg[1]])
    k.op("dve", lambda e: e.tensor_tensor(out=wTm[:], in0=wtmp.rearrange("p (g t) -> p g t", g=16),
                                          in1=cm.unsqueeze(1).broadcast_to([128, 16, 128]), op=ALU.mult),
         r=[r_vg[0], r_vg[1]], w=[r_wTm])

    for sl in range(4):
        slab, r_sl = load_slab(slabs, r_slabs, w_in_v[:, :, 2048 + sl * 512:2048 + (sl + 1) * 512])
        for t in range(8):
            bk = next_bank()
            mm_tm(bk, hTo, t * 128, slab, 512, r=[r_sl, rho[t]])
            k.op("act", lambda e: e.activation(out=vg[:, t, sl * 512:(sl + 1) * 512], in_=ps[bk][:, :], func=AF.Gelu), r=[r_ps[bk]], w=[r_vg[t]])
    qT_dv = qT_d.rearrange("h d t -> d h t")

    def q_chunk(i):
        sl, hp = i // 2, i % 2
        sb = sl % 2
        if hp == 0:
            q_chunk.cur = load_slab(slabs, r_slabs, w_in_v[:, :, 4096 + sl * 512:4096 + (sl + 1) * 512])
        slab, r_sl = q_chunk.cur
        for hh in (2 * hp, 2 * hp + 1):
            for half in range(2):
                bk = next_bank()
                mm_fm(bk, 512, slab, hh * 128, hTo, half * 512, r=[r_sl] + rho[half * 4:half * 4 + 4])
                k.op("act", lambda e: e.activation(out=qst[sb][:, hh, half * 512:(half + 1) * 512], in_=ps[bk][:, :], func=AF.Copy), r=[r_ps[bk]], w=[r_qst[sb]])
        if hp == 1:
            k.dma("sp", qT_dv[:, sl * 4:(sl + 1) * 4, :], qst[sb][:], r=[r_qst[sb]])

    for t in range(8):
        st, r_st = next_stat()
        for c in range(4):
            k.op("dve", lambda e, c=c: e.bn_stats(out=st6[:, c, :], in_=vg[:, t, c * 512:(c + 1) * 512]), r=[r_vg[t]], w=[r_st6])
        k.op("dve", lambda e: e.bn_aggr(out=st[:, 0:2], in_=st6[:]), r=[r_st6], w=[r_st])
        rstd_from_ss(st[:, 1:2], st[:, 2:3], r_st, 1.0)
        k.op("dve", lambda e: e.tensor_scalar(out=vg[:, t, :], in0=vg[:, t, :], scalar1=st[:, 0:1], scalar2=st[:, 2:3], op0=ALU.subtract, op1=ALU.mult),
             r=[r_st], w=[r_vg[t]])
        k.op("dve", lambda e: e.tensor_tensor(out=vln[:, t, :], in0=vg[:, t, :], in1=vgbc[:], op=ALU.mult), r=[r_vg[t], r_vgbc], w=[r_vln[t]])
        q_chunk(t)
    k.barrier()
    off = P0 + 32 * 1024
    mixT, off = T("mixT", [128, 16, TOK], BF16, off)
    r_mix = k.res()
    gu = [None, None]
    gu[0], off = T("gu0", [128, 512], F32, off)
    gu[1], off = T("gu1", [128, 512], F32, off)
    r_gu = [k.res(), k.res()]
    t2 = [None, None]
    t2[0], off = T("t20", [128, 512], F32, off)
    t2[1], off = T("t21", [128, 512], F32, off)
    r_t2 = [k.res(), k.res()]
    assert off <= offB1, (off, offB1)
    ui = 0
    for g in range(16):
        if g % 4 == 0:
            slab, r_sl = load_slab(slabs, r_slabs, w_in_v[:, :, (g // 4) * 512:(g // 4 + 1) * 512])
        for half in range(2):
            bu = next_bank()
            mm_fm(bu, 512, slab, (g % 4) * 128, hTo, half * 512, r=[r_sl] + rho[half * 4:half * 4 + 4])
            bs = next_bank()
            k.pe([(lambda e, t=t: e.matmul(ps[bs][:, (t % 4) * 128:(t % 4 + 1) * 128], lhsT=vln[:, t, g * 128:(g + 1) * 128], rhs=wTm[:, g, :], start=True, stop=True))
                  for t in range(half * 4, half * 4 + 4)], r=[r_vln[t] for t in range(half * 4, half * 4 + 4)] + [r_wTm], w=[r_ps[bs]])
            i = ui % 2
            ui += 1
            k.op("act", lambda e: e.activation(out=gu[i][:], in_=ps[bu][:, :], func=AF.Gelu), r=[r_ps[bu]], w=[r_gu[i]])
            k.op("dve", lambda e: e.tensor_tensor(out=t2[i][:].rearrange("p (a b) -> p a b", a=4), in0=ps[bs][:, :].rearrange("p (a b) -> p a b", a=4),
                                                  in1=bbc[:, g * 128:(g + 1) * 128].unsqueeze(1).broadcast_to([128, 4, 128]), op=ALU.add),
                 r=[r_ps[bs], r_bbc], w=[r_t2[i]])
            k.op("dve", lambda e: e.tensor_tensor(out=mixT[:, g, half * 512:(half + 1) * 512], in0=t2[i][:], in1=gu[i][:], op=ALU.mult),
                 r=[r_t2[i], r_gu[i]], w=[r_mix])
    k.dma("sp", hTo_d[:, :], hTo[:].rearrange("p c t -> p (c t)"), r=rho)
    k.dma("sp", mix_d[:, :], mixT[:].rearrange("p c t -> p (c t)"), r=[r_mix])
    if stop_after == "B":
        k.barrier()
        return nc

    k.barrier()
    off = P0
    attnT, off = T("attnT", [128, 16, TOK], BF16, off)
    r_att = k.res()
    offC0 = off
    BdT, off = T("BdT", [128, 16, 128], BF16, off)
    BpT, off = T("BpT", [128, 16, 128], BF16, off)
    Ecr, off = T("Ecr", [128, 16, 128], BF16, off)
    r_bias = k.res()
    pm_sb, off = T("pm_sb", [128, 128], F32, off)
    oh_sb, off = T("oh_sb", [128, 128], F32, off)
    r_pm = k.res()
    kTb, vab, qTb = [None, None], [None, None], [None, None]
    r_kv = [k.res(), k.res()]
    for b in range(2):
        kTb[b], off = T(f"kT{b}", [128, NVT], BF16, off)
        vab[b], off = T(f"va{b}", [128, NVT // 128, 130], BF16, off)
        qTb[b], off = T(f"qT{b}", [128, TOK], BF16, off)
    km, kmb, kml, gsm, sel, m8, selp, stmp = [[None, None] for _ in range(8)]
    r_km = [k.res(), k.res()]
    r_sel = [k.res(), k.res()]
    for b in range(2):
        km[b], off = T(f"km{b}", [128, 32], F32, off)
        kmb[b], off = T(f"kmb{b}", [128, 32], BF16, off)
        kml[b], off = T(f"kml{b}", [128, 32], BF16, off)
        gsm[b], off = T(f"gsm{b}", [128, 8, 32], F32, off)
        sel[b], off = T(f"sel{b}", [128, 8, 32], F32, off)
        m8[b], off = T(f"m8{b}", [128, 8, 8], F32, off)
        selp[b], off = T(f"selp{b}", [128, 8], F32, off)
        stmp[b], off = T(f"stmp{b}", [128, 32], F32, off)
    tma = [None] * 4
    r_tma = [k.res() for _ in range(4)]
    for i_ in range(4):
        tma[i_], off = T(f"tma{i_}", [128, 2, 132], F32, off)
    acc, off = T("acc", [128, 8, 132], F32, off)
    r_acc = [k.res() for _ in range(8)]
    PT = [None] * 4
    for i_ in range(4):
        PT[i_], off = T(f"PT{i_}", [128, 512], BF16, off)
    r_PT = [k.res() for _ in range(4)]
    PTl, off = T("PTl", [128, 512], BF16, off)
    r_PTl = k.res()
    obf, off = T("obf", [128, 8, 128], BF16, off)
    r_obf = k.res()
    rec, off = T("rec", [128, 8], F32, off)
    r_rec = k.res()
    tdg_sb, off2 = T("tdg_sb", [128, 16, 128], F32, off)
    tpv_sb, off2 = T("tpv_sb", [128, 16, 128], F32, off2)
    cf_sb, off2 = T("cf_sb", [128, 16], F32, off2)
    ng_sb, off2 = T("ng_sb", [128, 128], F32, off2)
    r_tb = k.res()
    assert off2 <= SB_LIMIT, off2

    k.dma("sp", tdg_sb[:].rearrange("p h q -> p (h q)"), tdg[:, :], w=[r_tb])
    k.dma("sp", tpv_sb[:].rearrange("p h q -> p (h q)"), tpv[:, :], w=[r_tb])
    k.dma("sp", cf_sb[:], cfar[:, :], w=[r_tb])
    k.dma("sp", ng_sb[:], negm[:, :], w=[r_tb])
    k.dma("sp", pm_sb[:], pmd[:, :], w=[r_pm])
    k.dma("sp", oh_sb[:], ohd[:, :], w=[r_pm])
    k.dma("sp", ksum[:].rearrange("p h n -> p (h n)"), ksum_d[:, :], w=[r_ksum])
    for h in range(NH):
        k.op("dve", lambda e, h=h: e.tensor_scalar(out=tdg_sb[:, h, :], in0=tdg_sb[:, h, :], scalar1=cf_sb[:, h:h + 1], scalar2=1.0 / SCALE, op0=ALU.subtract, op1=ALU.mult),
             r=[r_tb], w=[r_tb])
        k.op("dve", lambda e, h=h: e.tensor_scalar(out=tpv_sb[:, h, :], in0=tpv_sb[:, h, :], scalar1=cf_sb[:, h:h + 1], scalar2=1.0 / SCALE, op0=ALU.subtract, op1=ALU.mult),
             r=[r_tb], w=[r_tb])
    k.op("dve", lambda e: e.tensor_tensor(out=BdT[:], in0=tdg_sb[:], in1=ng_sb[:].unsqueeze(1).broadcast_to([128, 16, 128]), op=ALU.add), r=[r_tb], w=[r_bias])
    k.op("dve", lambda e: e.tensor_copy(out=BpT[:], in_=tpv_sb[:]), r=[r_tb], w=[r_bias])
    k.op("act", lambda e: e.activation(out=tpv_sb[:], in_=tpv_sb[:], func=AF.Exp, scale=SCALE), r=[r_tb], w=[r_tb])
    k.op("dve", lambda e: e.tensor_scalar(out=Ecr[:], in0=tpv_sb[:], scalar1=-1.0, scalar2=None, op0=ALU.add), r=[r_tb], w=[r_bias])
    for b in range(2):
        k.op("dve", lambda e, b=b: e.memset(vab[b][:, :, 128:130], 1.0), w=[r_kv[b]])
    for b in range(2):
        k.op("dve", lambda e, b=b: e.memset(km[b][:], 0.0), w=[r_km[b]])

    def head_loads(h):
        b = h % 2
        k.dma("sp", kTb[b][:], kT_d[h, :, :], w=[r_kv[b]])
        k.dma("sp", vab[b][:, :, 0:128], v_d[:, h * 128:(h + 1) * 128].rearrange("(n p) d -> p n d", p=128), w=[r_kv[b]])
        k.dma("sp", qTb[b][:], qT_d[h, :, :], w=[r_kv[b]])

    def gating(h):
        b = h % 2
        kT, qT, rkv = kTb[b], qTb[b], r_kv[b]
        kmh = ksum[:, h, 0:32]
        k.op("dve", lambda e: e.tensor_scalar(out=kmb[b][:], in0=kmh, scalar1=1.0 / LB, scalar2=None, op0=ALU.mult), r=[r_ksum], w=[r_km[b]])
        k.op("dve", lambda e: e.scalar_tensor_tensor(out=kml[b][:], in0=kmh, scalar=1.0 / LB, in1=kmb[b][:], op0=ALU.mult, op1=ALU.subtract), r=[r_ksum, r_km[b]], w=[r_km[b]])
        fns = []
        for t in range(8):
            fns.append(lambda e, t=t: e.matmul(ps[7][:, 256 + t * 32:256 + (t + 1) * 32], lhsT=qT[:, t * 128:(t + 1) * 128], rhs=kmb[b][:, 0:32], start=True, stop=False))
            fns.append(lambda e, t=t: e.matmul(ps[7][:, 256 + t * 32:256 + (t + 1) * 32], lhsT=qT[:, t * 128:(t + 1) * 128], rhs=kml[b][:, 0:32], start=False, stop=True))
        k.pe(fns, r=[rkv, r_km[b]], w=[r_ps[7]])
        k.op("dve", lambda e: e.tensor_tensor(out=gsm[b][:].rearrange("p (s t) n -> p s t n", s=4), in0=ps[7][:, 256:512].rearrange("p (s t n) -> p s t n", s=4, t=2),
                                              in1=pm_sb[:].rearrange("p (s n) -> p s n", s=4).unsqueeze(2).broadcast_to([128, 4, 2, 32]), op=ALU.add),
             r=[r_ps[7], r_pm], w=[r_sel[b]])
        for t in range(8):
            k.op("dve", lambda e, t=t: e.max(out=m8[b][:, t, :], in_=gsm[b][:, t, :]), r=[r_sel[b]], w=[r_sel[b]])
        k.op("dve", lambda e: e.tensor_scalar(out=m8[b][:, :, 2:3], in0=m8[b][:, :, 2:3], scalar1=-1.0e29, scalar2=None, op0=ALU.max), r=[r_sel[b]], w=[r_sel[b]])
        k.op("dve", lambda e: e.tensor_tensor(out=sel[b][:], in0=gsm[b][:], in1=m8[b][:, :, 2:3].broadcast_to([128, 8, 32]), op=ALU.is_ge), r=[r_sel[b]], w=[r_sel[b]])
        for s_ in range(4):
            k.op("dve", lambda e, s_=s_: e.tensor_tensor(out=stmp[b][:], in0=sel[b][:, 2 * s_, :], in1=oh_sb[:, s_ * 32:(s_ + 1) * 32], op=ALU.mult), r=[r_sel[b], r_pm], w=[r_sel[b]])
            k.op("dve", lambda e, s_=s_: e.tensor_reduce(out=selp[b][:, s_:s_ + 1], in_=stmp[b][:], axis=AX.X, op=ALU.add), r=[r_sel[b]], w=[r_sel[b]])

    head_loads(0)
    gating(0)
    sT_i = [0]
    tma_i = [0]
    for h in range(NH):
        b = h % 2
        kT, va, qT = kTb[b], vab[b], qTb[b]
        rkv = r_kv[b]
        selh, selph, r_selh = sel[b], selp[b], r_sel[b]
        if h + 1 < NH:
            head_loads(h + 1)
        k.op("pool", lambda e: e.memset(acc[:], 0.0), w=r_acc)

        blocks_ = [(s_, j) for s_ in range(4) for j in range(RS[s_])]
        SB = [0, 1, 6]

        def emit_qk(blk):
            s_, j = blk
            i_ = sT_i[0]
            sT_i[0] += 1
            sb_ = SB[i_ % 3]
            pt_ = i_ % 4
            k.pe([(lambda e, kt=kt: e.matmul(ps[sb_][:, kt * 256:(kt + 1) * 256], lhsT=kT[:, (2 * j + kt) * 128:(2 * j + kt + 1) * 128],
                                             rhs=qT[:, s_ * 256:(s_ + 1) * 256], start=True, stop=True)) for kt in range(2)],
                 r=[rkv], w=[r_ps[sb_]])
            k.op("act", lambda e: e.activation(out=PT[pt_][:], in_=ps[sb_][:, :], func=AF.Exp, scale=SCALE), r=[r_ps[sb_]], w=[r_PT[pt_]])
            return pt_

        def emit_pv(blk, pt_, n_):
            s_, j = blk
            ob = 2 + (n_ % 4)
            fns = []
            for qt in range(2):
                for kt in range(2):
                    fns.append(lambda e, qt=qt, kt=kt: e.matmul(ps[ob][:, qt * 256:qt * 256 + 129], lhsT=PT[pt_][:, kt * 256 + qt * 128:kt * 256 + (qt + 1) * 128],
                                                                rhs=va[:, 2 * j + kt, 0:129], start=(kt == 0), stop=(kt == 1)))
            k.pe(fns, r=[r_PT[pt_], rkv], w=[r_ps[ob]])
            use_act = ACT_SHARE > 0 and (n_ % ACT_SHARE) == ACT_SHARE - 1
            for qt in range(2):
                tl = s_ * 2 + qt
                src_ = ps[ob][:, qt * 256:qt * 256 + 129]
                if use_act:
                    x_ = tma_i[0] % 4
                    tma_i[0] += 1
                    k.op("act", lambda e: e.activation(out=tma[x_][:, 0, 0:129], in_=src_, func=AF.Copy, scale=selh[:, tl, j:j + 1]),
                         r=[r_ps[ob], r_selh], w=[r_tma[x_]])
                    k.op("pool", lambda e: e.tensor_tensor(out=acc[:, tl, 0:129], in0=acc[:, tl, 0:129], in1=tma[x_][:, 0, 0:129], op=ALU.add),
                         r=[r_tma[x_]], w=[r_acc[tl]])
                else:
                    k.op("dve", lambda e: e.scalar_tensor_tensor(out=acc[:, tl, 0:129], in0=src_, scalar=selh[:, tl, j:j + 1], in1=acc[:, tl, 0:129], op0=ALU.mult, op1=ALU.add),
                         r=[r_ps[ob], r_selh], w=[r_acc[tl]])

        def emit_qk_local(s):
            LTI = [16 * s + 13, 16 * s + 14, 16 * s + 15]
            Q0, Q1 = 2 * s, 2 * s + 1
            i_ = sT_i[0]
            sT_i[0] += 1
            sb_ = SB[i_ % 3]
            pt_ = i_ % 4

            def kt_(i):
                return kT[:, LTI[i] * 128:(LTI[i] + 1) * 128]

            def q_(i):
                return qT[:, i * 128:(i + 1) * 128]

            k.pe([lambda e: e.matmul(ps[sb_][:, 0:128], lhsT=kt_(1), rhs=q_(Q0), start=True, stop=False),
                  lambda e: e.matmul(ps[sb_][:, 0:128], lhsT=ident[:], rhs=BdT[:, h, :], start=False, stop=True),
                  lambda e: e.matmul(ps[sb_][:, 128:256], lhsT=kt_(1), rhs=q_(Q1), start=True, stop=False),
                  lambda e: e.matmul(ps[sb_][:, 128:256], lhsT=ident[:], rhs=BpT[:, h, :], start=False, stop=True),
                  lambda e: e.matmul(ps[sb_][:, 256:384], lhsT=kt_(2), rhs=q_(Q1), start=True, stop=False),
                  lambda e: e.matmul(ps[sb_][:, 256:384], lhsT=ident[:], rhs=BdT[:, h, :], start=False, stop=True),
                  lambda e: e.matmul(ps[sb_][:, 384:512], lhsT=kt_(0), rhs=q_(Q0), start=True, stop=True)],
                 r=[rkv, r_ident, r_bias], w=[r_ps[sb_]])
            k.op("act", lambda e: e.activation(out=PT[pt_][:], in_=ps[sb_][:, :], func=AF.Exp, scale=SCALE), r=[r_ps[sb_]], w=[r_PT[pt_]])
            k.op("dve", lambda e: e.tensor_tensor(out=PT[pt_][:, 384:512], in0=PT[pt_][:, 384:512], in1=Ecr[:, h, :], op=ALU.mult), r=[r_bias], w=[r_PT[pt_]])
            return pt_

        def emit_pv_local(s, pt_, n_):
            LTI = [16 * s + 13, 16 * s + 14, 16 * s + 15]
            Q0, Q1 = 2 * s, 2 * s + 1
            ob, ob2 = 2 + (n_ % 4), 2 + ((n_ + 1) % 4)
            PTl = PT[pt_]
            k.pe([lambda e: e.matmul(ps[ob][:, 0:129], lhsT=PTl[:, 0:128], rhs=va[:, LTI[1], 0:129], start=True, stop=True),
                  lambda e: e.matmul(ps[ob][:, 256:385], lhsT=PTl[:, 128:256], rhs=va[:, LTI[1], 0:129], start=True, stop=False),
                  lambda e: e.matmul(ps[ob][:, 256:385], lhsT=PTl[:, 256:384], rhs=va[:, LTI[2], 0:129], start=False, stop=True),
                  lambda e: e.matmul(ps[ob2][:, 0:129], lhsT=PTl[:, 384:512], rhs=va[:, LTI[0], 0:129], start=True, stop=True)],
                 r=[r_PT[pt_], rkv], w=[r_ps[ob], r_ps[ob2]])
            k.op("dve", lambda e: e.tensor_tensor(out=acc[:, Q0, 0:129], in0=acc[:, Q0, 0:129], in1=ps[ob][:, 0:129], op=ALU.add), r=[r_ps[ob]], w=[r_acc[Q0]])
            k.op("dve", lambda e: e.tensor_tensor(out=acc[:, Q1, 0:129], in0=acc[:, Q1, 0:129], in1=ps[ob][:, 256:385], op=ALU.add), r=[r_ps[ob]], w=[r_acc[Q1]])
            k.op("dve", lambda e: e.scalar_tensor_tensor(out=acc[:, Q0, 0:129], in0=ps[ob2][:, 0:129], scalar=selph[:, s:s + 1], in1=acc[:, Q0, 0:129], op0=ALU.mult, op1=ALU.add),
                 r=[r_ps[ob2], r_selh], w=[r_acc[Q0]])

        items = [("l", s_) for s_ in range(4)] + [("p", blk) for blk in blocks_]
        pend = []
        n_ = 0
        for ii, (kind, arg) in enumerate(items):
            if kind == "p":
                pend.append((kind, arg, emit_qk(arg), n_))
                n_ += 1
            else:
                pend.append((kind, arg, emit_qk_local(arg), n_))
                n_ += 2
            if len(pend) > PD:
                kd, a_, pt_, nn = pend.pop(0)
                (emit_pv if kd == "p" else emit_pv_local)(a_, pt_, nn)
            if ii == 12 and h + 1 < NH:
                gating(h + 1)
        while pend:
            kd, a_, pt_, nn = pend.pop(0)
            (emit_pv if kd == "p" else emit_pv_local)(a_, pt_, nn)

        k.op("dve", lambda e: e.reciprocal(out=rec[:], in_=acc[:, :, 128]), r=r_acc, w=[r_rec])
        k.op("dve", lambda e: e.tensor_tensor(out=obf[:], in0=acc[:, :, 0:128], in1=rec[:].unsqueeze(2).broadcast_to([128, 8, 128]), op=ALU.mult), r=r_acc + [r_rec], w=[r_obf])
        pb = ps[7][:].bitcast(BF16)
        for hf in range(2):
            k.pe([(lambda e, t=t: e.transpose(out=pb[:, (t % 4) * 128:(t % 4 + 1) * 128], in_=obf[:, t, :], identity=ident[:])) for t in range(hf * 4, hf * 4 + 4)],
                 r=[r_obf, r_ident], w=[r_ps[7]])
            k.op("act", lambda e: e.activation(out=attnT[:, h, hf * 512:(hf + 1) * 512], in_=pb[:, 0:512], func=AF.Copy), r=[r_ps[7]], w=[r_att])
    if dbg:
        k.dma("sp", att_d[:, :], attnT[:].rearrange("p c t -> p (c t)"), r=[r_att])
    if stop_after == "C":
        k.barrier()
        return nc

    k.barrier()
    off = offC0
    mergedT, off = T("mergedT", [128, 16, TOK], BF16, off)
    r_mer = k.res()
    offD1 = off
    hTo, off = T("hToD", [128, 16, TOK], BF16, off)
    mixT, off = T("mixTD", [128, 16, TOK], BF16, off)
    r_ho, r_mi = k.res(), k.res()
    wu = [None, None]
    wu[0], off = T("wu0", [128, 16, 4, 256], BF16, off)
    wu[1], off = T("wu1", [128, 16, 4, 256], BF16, off)
    r_wu = [k.res(), k.res()]
    sa, off = T("sa", [128, 512], F32, off)
    sbb, off = T("sb", [128, 512], F32, off)
    r_sa, r_sb = k.res(), k.res()
    assert off <= SB_LIMIT, off
    k.dma("sp", hTo[:].rearrange("p c t -> p (c t)"), hTo_d[:, :], w=[r_ho])
    k.dma("sp", mixT[:].rearrange("p c t -> p (c t)"), mix_d[:, :], w=[r_mi])
    it = 0
    for jj in range(8):
        b = jj % 2
        for i, view in enumerate((w_pa_v[:, :, jj * 256:(jj + 1) * 256], w_pb_v[:, :, jj * 256:(jj + 1) * 256],
                                  w_in_v[:, :, 10240 + jj * 256:10240 + (jj + 1) * 256], w_in_v[:, :, 12288 + jj * 256:12288 + (jj + 1) * 256])):
            k.dma("pool", wu[b][:, :, i, :], view, w=[r_wu[b]])
        for jh in range(2):
            j = jj * 2 + jh
            for half in range(2):
                bs = 4 * (it % 2)
                it += 1
                a0 = half * 512
                for i, (act_, r_act) in enumerate(((mixT, r_mi), (attnT, r_att), (hTo, r_ho), (hTo, r_ho))):
                    k.pe([(lambda e, c=c, i=i, act_=act_: e.matmul(ps[bs + i][:, 0:512], lhsT=wu[b][:, c, i, jh * 128:(jh + 1) * 128], rhs=act_[:, c, a0:a0 + 512], start=(c == 0), stop=(c == 15)))
                          for c in range(16)], r=[r_wu[b], r_act], w=[r_ps[bs + i]])
                k.op("act", lambda e: e.activation(out=sa[:], in_=ps[bs + 2][:, :], func=AF.Sigmoid), r=[r_ps[bs + 2]], w=[r_sa])
                k.op("act", lambda e: e.activation(out=sbb[:], in_=ps[bs + 3][:, :], func=AF.Sigmoid), r=[r_ps[bs + 3]], w=[r_sb])
                k.op("dve", lambda e: e.tensor_tensor(out=sa[:], in0=sa[:], in1=ps[bs + 0][:, :], op=ALU.mult), r=[r_ps[bs + 0]], w=[r_sa])
                k.op("dve", lambda e: e.tensor_tensor(out=sbb[:], in0=sbb[:], in1=ps[bs + 1][:, :], op=ALU.mult), r=[r_ps[bs + 1]], w=[r_sb])
                k.op("dve", lambda e: e.tensor_tensor(out=mergedT[:, j, a0:a0 + 512], in0=sa[:], in1=sbb[:], op=ALU.add), r=[r_sa, r_sb], w=[r_mer])
    k.barrier()
    off = offD1
    x2, off = T("x2", [128, 8, D], F32, off)
    r_x2 = [k.res() for _ in range(8)]
    offX = off
    slabs = [None, None]
    slabs[0], off = T("slabD0", [128, 16, 512], BF16, off)
    slabs[1], off = T("slabD1", [128, 16, 512], BF16, off)
    r_slabs = [k.res(), k.res()]
    offS = off
    assert off <= SB_LIMIT, off
    for t in range(8):
        k.dma("sp", x2[:, t, :], xo[t * 128:(t + 1) * 128, :], w=[r_x2[t]])
    for sl in range(4):
        slab, r_sl = load_slab(slabs, r_slabs, w_out_v[:, :, sl * 512:(sl + 1) * 512])
        for t in range(8):
            bk = next_bank(8)
            mm_tm(bk, mergedT, t * 128, slab, 512, r=[r_sl, r_mer])
            k.op("dve", lambda e: e.tensor_tensor(out=x2[:, t, sl * 512:(sl + 1) * 512], in0=x2[:, t, sl * 512:(sl + 1) * 512], in1=ps[bk][:, :], op=ALU.add),
                 r=[r_ps[bk]], w=[r_x2[t]])
    if dbg:
        k.dma("sp", x2_d[:, :], x2[:].rearrange("p t d -> p (t d)"), r=r_x2)
    if stop_after == "D":
        k.barrier()
        return nc

    pre_slab = load_slab(slabs, r_slabs, w_up_v[:, :, 0:512])
    k.barrier()
    k.dma("sp", gbc[:], g_mlp[:, :], w=[r_gbc])
    off = offC0
    h2T, off = T("h2T", [128, 16, TOK], BF16, off)
    r_h2 = [k.res() for _ in range(8)]
    assert off <= offD1
    off = offS
    upT, off = T("upT", [128, 16, TOK], BF16, off)
    r_up = k.res()
    assert off <= SB_LIMIT, off
    off = P0
    xn[0], off = T("xnE0", [128, D], BF16, off)
    xn[1], off = T("xnE1", [128, D], BF16, off)
    junk, off = T("junkE", [128, D], BF16, off)
    rl = [None, None]
    rl[0], off = T("rl0", [128, 512], F32, off)
    rl[1], off = T("rl1", [128, 512], F32, off)
    r_rl = [k.res(), k.res()]
    assert off <= offC0, off
    tmpE = ([None, None], [None, None], xn, [k.res(), k.res()], junk, k.res())

    def src_x2(t, xb, rxb):
        return x2[:, t, :], r_x2[t]

    prep_hT(src_x2, 8, h2T, r_h2, tmpE)
    ri = 0
    for fg in range(4):
        for sl in range(4):
            if fg == 0 and sl == 0:
                slab, r_sl = pre_slab
            else:
                slab, r_sl = load_slab(slabs, r_slabs, w_up_v[:, :, (fg * 4 + sl) * 512:(fg * 4 + sl + 1) * 512])
            for fc in range(4):
                for half in range(2):
                    bk = next_bank()
                    mm_fm(bk, 512, slab, fc * 128, h2T, half * 512, r=[r_sl] + r_h2[half * 4:half * 4 + 4])
                    i = ri % 2
                    ri += 1
                    k.op("act", lambda e: e.activation(out=rl[i][:], in_=ps[bk][:, :], func=AF.Relu), r=[r_ps[bk]], w=[r_rl[i]])
                    k.op("dve", lambda e: e.tensor_tensor(out=upT[:, sl * 4 + fc, half * 512:(half + 1) * 512], in0=rl[i][:], in1=rl[i][:], op=ALU.mult), r=[r_rl[i]], w=[r_up])
        wd_v = w_down[fg * 2048:(fg + 1) * 2048, :].rearrange("(c p) n -> p c n", p=128)
        for sl in range(4):
            slab, r_sl = load_slab(slabs, r_slabs, wd_v[:, :, sl * 512:(sl + 1) * 512])
            for t in range(8):
                bk = next_bank()
                mm_tm(bk, upT, t * 128, slab, 512, r=[r_sl, r_up])
                k.op("dve", lambda e: e.tensor_tensor(out=x2[:, t, sl * 512:(sl + 1) * 512], in0=x2[:, t, sl * 512:(sl + 1) * 512], in1=ps[bk][:, :], op=ALU.add),
                     r=[r_ps[bk]], w=[r_x2[t]])

    gfin, _ = T("gfin", [128, D], F32, P0)
    r_gfin = k.res()
    k.dma("sp", gfin[:], g_fin[:, :], w=[r_gfin, tmpE[3][0], tmpE[3][1]])
    off = offC0
    ot = [None, None]
    ot[0], off = T("ot0", [128, D], F32, off)
    ot[1], off = T("ot1", [128, D], F32, off)
    r_ot = [k.res(), k.res()]
    for t in range(8):
        st, r_st = next_stat()
        k.op("dve", lambda e: e.memset(st[:, 0:2], 0.0), w=[r_st])
        k.op("act", lambda e: e.activation(out=junk[:], in_=x2[:, t, :], func=AF.Square, accum_out=st[:, 0:1]), r=[r_x2[t]], w=[tmpE[5], r_st])
        rstd_from_ss(st[:, 0:1], st[:, 1:2], r_st, D)
        b = t % 2
        k.op("dve", lambda e: e.scalar_tensor_tensor(out=ot[b][:], in0=x2[:, t, :], scalar=st[:, 1:2], in1=gfin[:], op0=ALU.mult, op1=ALU.mult),
             r=[r_x2[t], r_st, r_gfin], w=[r_ot[b]] + (r_h2 if t < 2 else []))
        k.dma("sp", out[t * 128:(t + 1) * 128, :], ot[b][:], r=[r_ot[b]])
    k.barrier()
    return nc


def _t5_bucket(n):
    n = np.maximum(n, 0)
    nf = np.maximum(n, 1).astype(np.float32)
    large = 16 + (np.log(nf / np.float32(16)) / np.float32(math.log(128 / 16)) * np.float32(16)).astype(np.int32)
    large = np.minimum(large, 31)
    return np.where(n < 16, n, large)


def core_blocks(c):
    return [c, 8 + c, 16 + c, 24 + c]


def make_inputs(c, x, ln_mix, w_in, a_v_gain, a_spatial, a_spatial_bias, w_proj_a, w_proj_b,
                w_out, rel_bias, ln_mlp, w_up, w_down, ln_final, shared=None):
    x2d = np.asarray(x, np.float32).reshape(S, D)
    blocks = core_blocks(c)
    if shared is None:
        shared = {}
        rep = lambda v: np.ascontiguousarray(np.broadcast_to(np.asarray(v, np.float32).reshape(1, -1), (128, np.asarray(v).size)))
        shared["w_in"] = np.ascontiguousarray(np.asarray(w_in, np.float32)[0])
        shared["w_pa"] = np.ascontiguousarray(np.asarray(w_proj_a, np.float32)[0])
        shared["w_pb"] = np.ascontiguousarray(np.asarray(w_proj_b, np.float32)[0])
        shared["w_out"] = np.ascontiguousarray(np.asarray(w_out, np.float32)[0])
        shared["w_up"] = np.ascontiguousarray(np.asarray(w_up, np.float32)[0])
        shared["w_down"] = np.ascontiguousarray(np.asarray(w_down, np.float32)[0])
        shared["g_mix"] = rep(np.asarray(ln_mix)[0])
        shared["g_v"] = rep(np.asarray(a_v_gain)[0])
        shared["g_mlp"] = rep(np.asarray(ln_mlp)[0])
        shared["g_fin"] = rep(np.asarray(ln_final))
        asp = np.asarray(a_spatial, np.float32)[0]
        shared["wsT"] = np.ascontiguousarray(asp.transpose(2, 0, 1).reshape(128, 16 * 128))
        s_i = np.arange(128)[:, None]
        t_i = np.arange(128)[None, :]
        shared["cmask"] = (s_i <= t_i).astype(np.float32)
        bs = np.asarray(a_spatial_bias, np.float32)[0]
        shared["bsb"] = np.ascontiguousarray(np.broadcast_to(bs.reshape(1, 16 * 128), (128, 16 * 128)))
        rb = np.asarray(rel_bias, np.float32)
        kk = np.arange(128)[:, None]
        qq = np.arange(128)[None, :]
        bd = _t5_bucket(qq - kk)
        td = rb[bd, :]
        td = np.where((qq >= kk)[:, :, None], td, rb[31][None, None, :])
        shared["tdg"] = np.ascontiguousarray(td.transpose(0, 2, 1).reshape(128, 16 * 128))
        bp = _t5_bucket(qq - kk + 128)
        tp = rb[bp, :]
        shared["tpv"] = np.ascontiguousarray(tp.transpose(0, 2, 1).reshape(128, 16 * 128))
        shared["cfar"] = np.ascontiguousarray(np.broadcast_to(rb[31].reshape(1, 16), (128, 16)))
        shared["negm"] = np.where(qq >= kk, 0.0, NEGV).astype(np.float32)
        shared["idn"] = np.eye(128, dtype=np.float32).astype(ml_dtypes.bfloat16)
    m = dict(shared)
    vorder = [None] * 32
    fixed = set()
    for s in range(4):
        vorder[8 * s + 7] = blocks[s]
        fixed.add(blocks[s])
        if blocks[s] > 0:
            vorder[8 * s + 6] = blocks[s] - 1
            fixed.add(blocks[s] - 1)
    rest = [r for r in range(32) if r not in fixed]
    for p in range(32):
        if vorder[p] is None:
            vorder[p] = rest.pop(0)
    assert sorted(vorder) == list(range(32))
    for s in range(4):
        assert all(vorder.index(r) < 8 * s + 7 for r in range(blocks[s]))
    rows = [x2d[r * LB:(r + 1) * LB] for r in vorder]
    m["xa"] = np.ascontiguousarray(np.concatenate(rows, 0))
    m["xo"] = np.ascontiguousarray(np.concatenate([x2d[b * LB:(b + 1) * LB] for b in blocks], 0))
    pm = np.zeros((4, 32), np.float32)
    oh = np.zeros((4, 32), np.float32)
    for s, b in enumerate(blocks):
        for v, r in enumerate(vorder):
            if r >= b:
                pm[s, v] = -1.0e30
            if r == b - 1:
                oh[s, v] = 1.0
    m["pm"] = np.ascontiguousarray(np.broadcast_to(pm.reshape(1, 128), (128, 128)))
    m["oh"] = np.ascontiguousarray(np.broadcast_to(oh.reshape(1, 128), (128, 128)))
    return m, shared


_NC_CACHE = {}


def kernel(**inputs):
    if "nc" not in _NC_CACHE:
        _NC_CACHE["nc"] = build()
    nc = _NC_CACHE["nc"]
    in_maps = []
    shared = None
    for c in range(8):
        m, shared = make_inputs(c, shared=shared, **inputs)
        in_maps.append(m)
    res = run_bass_kernel_spmd(nc, in_maps, core_ids=list(range(8)))
    outp = np.zeros((S, D), np.float32)
    for c in range(8):
        o = np.asarray(res.results[c]["out"], np.float32)
        for s, b in enumerate(core_blocks(c)):
            outp[b * LB:(b + 1) * LB] = o[s * LB:(s + 1) * LB]
    return outp.reshape(1, S, D)
```
